# Optimizing a Trainium2 kernel written in Bass

```python
import math
import jax, jax.numpy as jnp
from jax import lax
import numpy as np

D_MODEL = 2048
BATCH = 4
SEQ = 2048
DEPTH = 1

CHUNK = 64
PLE_DIM = 256
D_MIX = D_MODEL
RET_WIDTH = D_MIX // 2
DIFF_WIDTH = D_MIX - RET_WIDTH
RET_HEADS = 8
RET_DV = RET_WIDTH // RET_HEADS
RET_DK = RET_DV // 2
DIFF_HEADS = 8
DIFF_DV = DIFF_WIDTH // DIFF_HEADS
DIFF_DH = DIFF_DV // 2
Q_BLOCK = 128
ROPE_BASE = 10000.0
EPS = 1e-6
RET_Q_COLS = RET_HEADS * RET_DK
RET_K_COLS = RET_HEADS * RET_DK
RET_V_COLS = RET_WIDTH
RET_G_COLS = RET_WIDTH
DIFF_Q_COLS = DIFF_HEADS * 2 * DIFF_DH
DIFF_K_COLS = DIFF_HEADS * 2 * DIFF_DH
DIFF_V_COLS = DIFF_WIDTH
DIFF_G_COLS = DIFF_WIDTH
D_IN = RET_Q_COLS + RET_K_COLS + RET_V_COLS + RET_G_COLS + DIFF_Q_COLS + DIFF_K_COLS + DIFF_V_COLS + DIFF_G_COLS

kernel_name = 'hybrid_retention_diffattn_layer'

F32 = jnp.float32


def rmsnorm(t, g):
    tf = t.astype(F32)
    out = tf * lax.rsqrt(jnp.mean(tf * tf, axis=-1, keepdims=True) + EPS) * g.astype(F32)
    return out.astype(t.dtype)


def rope(t, cos, sin):
    half = t.shape[-1] // 2
    t1, t2 = t[..., :half], t[..., half:]
    c = cos[None, :, None, :]
    s = sin[None, :, None, :]
    return jnp.concatenate([t1 * c - t2 * s, t1 * s + t2 * c], axis=-1)


def retention(q, k, v):
    b, s = q.shape[:2]
    n = s // CHUNK
    log_g = jnp.log1p(-jnp.exp2(-5.0 - jnp.arange(RET_HEADS, dtype=F32)))
    idx = jnp.arange(CHUNK, dtype=F32)
    d_intra = jnp.exp(jnp.abs(idx[:, None] - idx[None, :])[None] * log_g[:, None, None])
    xi = jnp.exp((idx + 1.0)[None, :] * log_g[:, None])
    zeta = jnp.exp((CHUNK - 1.0 - idx)[None, :] * log_g[:, None])
    g_chunk = jnp.exp(CHUNK * log_g)
    qc = q.astype(F32).reshape(b, n, CHUNK, RET_HEADS, RET_DK)
    kc = k.astype(F32).reshape(b, n, CHUNK, RET_HEADS, RET_DK)
    vc = v.astype(F32).reshape(b, n, CHUNK, RET_HEADS, RET_DV)
    scores = jnp.einsum('bnihd,bnjhd->bnhij', qc, kc) * d_intra
    intra = jnp.einsum('bnhij,bnjhe->bnihe', scores, vc)
    kv = jnp.einsum('bnjhd,bnjhe,hj->nbhde', kc, vc, zeta)

    def step(state, kv_n):
        return state * g_chunk[None, :, None, None] + kv_n, state

    _, r_prev = lax.scan(step, jnp.zeros((b, RET_HEADS, RET_DK, RET_DV), F32), kv)
    cross = jnp.einsum('bnihd,nbhde->bnihe', qc, r_prev) * xi.T[None, None, :, :, None]
    return (intra + cross).reshape(b, s, RET_HEADS, RET_DV)


def diff_attention(q, k, v, lam):
    b, s = q.shape[:2]
    nb = s // Q_BLOCK
    scale = DIFF_DH ** -0.5
    key_chunk = jnp.arange(s) // CHUNK
    kf = k.astype(F32)
    vf = v.astype(F32)
    qb = (q.astype(F32) * scale).reshape(b, nb, Q_BLOCK, DIFF_HEADS, 2, DIFF_DH).transpose(1, 0, 2, 3, 4, 5)

    def block(args):
        q_blk, blk = args
        q_chunk = (blk * Q_BLOCK + jnp.arange(Q_BLOCK)) // CHUNK
        mask = key_chunk[None, :] <= q_chunk[:, None]
        sc = jnp.einsum('bqhcd,bkhcd->bhcqk', q_blk, kf)
        sc = jnp.where(mask, sc, -jnp.inf)
        pr = jax.nn.softmax(sc, axis=-1)
        attn = pr[:, :, 0] - lam * pr[:, :, 1]
        return jnp.einsum('bhqk,bkhe->bqhe', attn, vf)

    out = lax.map(block, (qb, jnp.arange(nb)))
    return out.transpose(1, 0, 2, 3, 4).reshape(b, s, DIFF_HEADS, DIFF_DV)


def setup_inputs(seed: int = 0) -> dict:
    key = jax.random.key(seed)
    ks = jax.random.split(key, 20)
    nrm = lambda k_, shape: jax.random.normal(k_, shape, F32)
    return {
        'x': nrm(ks[0], (BATCH, SEQ, D_MODEL)),
        'p': nrm(ks[1], (DEPTH, BATCH, SEQ, PLE_DIM)),
        'attn_norm': 1.0 + 0.01 * nrm(ks[2], (DEPTH, D_MODEL)),
        'w_in': nrm(ks[3], (DEPTH, D_MODEL, D_IN)) * D_MODEL ** -0.5,
        'ret_gn': 1.0 + 0.01 * nrm(ks[4], (DEPTH, RET_WIDTH)),
        'diff_qn': 1.0 + 0.01 * nrm(ks[5], (DEPTH, DIFF_DH)),
        'diff_kn': 1.0 + 0.01 * nrm(ks[6], (DEPTH, DIFF_DH)),
        'diff_lq1': 0.1 * nrm(ks[7], (DEPTH, DIFF_DH)),
        'diff_lk1': 0.1 * nrm(ks[8], (DEPTH, DIFF_DH)),
        'diff_lq2': 0.1 * nrm(ks[9], (DEPTH, DIFF_DH)),
        'diff_lk2': 0.1 * nrm(ks[10], (DEPTH, DIFF_DH)),
        'diff_subln': 1.0 + 0.01 * nrm(ks[11], (DEPTH, DIFF_DV)),
        'w_out': nrm(ks[12], (DEPTH, D_MIX, D_MODEL)) * D_MIX ** -0.5,
        'ple_norm': 1.0 + 0.01 * nrm(ks[13], (DEPTH, D_MODEL)),
        'w_ple_gate': nrm(ks[14], (DEPTH, D_MODEL, D_MODEL)) * D_MODEL ** -0.5,
        'w_ple_proj': nrm(ks[15], (DEPTH, PLE_DIM, D_MODEL)) * PLE_DIM ** -0.5,
    }


def reference(x, p, attn_norm, w_in, ret_gn, diff_qn, diff_kn, diff_lq1, diff_lk1, diff_lq2, diff_lk2, diff_subln, w_out, ple_norm, w_ple_gate, w_ple_proj):
    b, s, _ = x.shape
    pos = jnp.arange(s, dtype=F32)
    inv_freq = ROPE_BASE ** (-jnp.arange(RET_DK // 2, dtype=F32) / (RET_DK // 2))
    ang = pos[:, None] * inv_freq[None, :]
    cos, sin = jnp.cos(ang), jnp.sin(ang)
    o1 = RET_Q_COLS
    o2 = o1 + RET_K_COLS
    o3 = o2 + RET_V_COLS
    o4 = o3 + RET_G_COLS
    o5 = o4 + DIFF_Q_COLS
    o6 = o5 + DIFF_K_COLS
    o7 = o6 + DIFF_V_COLS
    h = x
    for i in range(DEPTH):
        u = rmsnorm(h, attn_norm[i])
        z = u @ w_in[i]
        rq, rk, rv, rg, dq, dk, dv, dg = jnp.split(z, [o1, o2, o3, o4, o5, o6, o7], axis=-1)
        rq = rope(rq.reshape(b, s, RET_HEADS, RET_DK).astype(F32), cos, sin)
        rk = rope(rk.reshape(b, s, RET_HEADS, RET_DK).astype(F32), cos, sin) * RET_DK ** -0.5
        ro = retention(rq, rk, rv.reshape(b, s, RET_HEADS, RET_DV))
        ro = rmsnorm(ro, ret_gn[i].reshape(RET_HEADS, RET_DV)).reshape(b, s, RET_WIDTH)
        ro = ro * jax.nn.silu(rg.astype(F32))
        lam_init = 0.8 - 0.6 * math.exp(-0.3 * i)
        lam = (jnp.exp(jnp.sum(diff_lq1[i].astype(F32) * diff_lk1[i].astype(F32)))
               - jnp.exp(jnp.sum(diff_lq2[i].astype(F32) * diff_lk2[i].astype(F32))) + lam_init)
        dq = rmsnorm(dq.reshape(b, s, DIFF_HEADS, 2, DIFF_DH), diff_qn[i])
        dk = rmsnorm(dk.reshape(b, s, DIFF_HEADS, 2, DIFF_DH), diff_kn[i])
        do = diff_attention(dq, dk, dv.reshape(b, s, DIFF_HEADS, DIFF_DV), lam)
        do = rmsnorm(do, diff_subln[i]) * (1.0 - lam_init)
        do = do.reshape(b, s, DIFF_WIDTH) * jax.nn.silu(dg.astype(F32))
        mixed = jnp.concatenate([ro, do], axis=-1).astype(h.dtype)
        h = h + mixed @ w_out[i]
        gate = jax.nn.sigmoid(rmsnorm(h, ple_norm[i]) @ w_ple_gate[i])
        h = h + gate * (p[i] @ w_ple_proj[i])
    return h
```

```python
import contextlib
import numpy as np
import concourse.bass as bass
import concourse.mybir as mybir
from concourse.bass_utils import run_bass_kernel_spmd

F32 = mybir.dt.float32
JUNK = object()
BF16 = mybir.dt.bfloat16
AF = mybir.ActivationFunctionType
ALU = mybir.AluOpType
AX = mybir.AxisListType

EPS = 1e-6
LAM_INIT = 0.8 - 0.6 * 1.0

_CST = {}
_off = 0
for _n, _w in [("an_g", 16), ("pn_g", 16), ("retgn", 8), ("subln", 1), ("qn2", 1), ("kn2", 1),
               ("lq1", 64), ("lk1", 64), ("lq2", 64), ("lk2", 64), ("cos", 512), ("sin", 512),
               ("DT", 1024), ("XiT", 512), ("zeta", 8), ("g128", 4), ("flag", 16), ("ident", 128)]:
    _CST[_n] = (_off, _w)
    _off += _w
NCST = _off


class Sched:
    ENG = ["pe", "act", "dve", "pool", "sp"]

    def __init__(self, eng_sems, dma_sems):
        self.sem = dict(eng_sems)
        self.ndma = len(dma_sems)
        for i, s in enumerate(dma_sems):
            self.sem[("d", i)] = s
        self.stream = {e: [] for e in self.ENG}
        self.cnt = {k: 0 for k in self.sem}
        self.waited = {e: {} for e in self.ENG}
        self.lastw = {}
        self.readers = {}
        self.dma_next = {"sp": 0, "pool": 0}
        self.bar = []
        self.nbar = 0

    def barrier(self, trivial):
        keys = []
        for e, (fn, extra_w) in trivial.items():
            k = ("bar", self.nbar, e)
            self.op(e, fn, [], [k] + list(extra_w))
            keys.append(k)
        self.nbar += 1
        self.bar = keys

    def _deps(self, eng, reads, writes):
        deps = {}

        def need(k, c):
            if c > 0 and deps.get(k, 0) < c:
                deps[k] = c
        for r in reads:
            lw = self.lastw.get(r)
            if lw:
                need(*lw)
            if r in self.BANKS:
                for k, c in self.readers.get(r, {}).items():
                    if k != eng:
                        need(k, c)
        for w in writes:
            lw = self.lastw.get(w)
            if lw and (lw[0] != eng or eng != "pe"):
                need(*lw)
            for k, c in self.readers.get(w, {}).items():
                if k != eng or eng != "pe":
                    need(k, c)
        waits = []
        for k, c in deps.items():
            if self.waited[eng].get(k, 0) < c:
                self.waited[eng][k] = c
                waits.append((k, c))
        return waits

    @staticmethod
    def bankof(k):
        if not isinstance(k, tuple):
            return None
        n = k[0]
        if n == "zps":
            return "Z%d" % k[1]
        if n == "tps":
            return "T"
        if n in ("sps", "pps"):
            return ("A", "B", "Z1")[k[1]]
        if n == "pA":
            return "A"
        if n == "pB":
            return "B"
        if n == "kv":
            return "O0"
        if n == "Ob":
            return "O%d" % k[1]
        return None

    BANKS = ("Z0", "Z1", "T", "A", "B", "O0", "O1", "O2")

    def _norm(self, reads, writes):
        r2 = [self.bankof(k) or k for k in reads]
        w2 = [self.bankof(k) or k for k in writes]
        return r2, w2

    def op(self, eng, fn, reads=(), writes=(), inc=True):
        reads, writes = self._norm(reads, writes)
        reads = list(reads) + self.bar
        waits = self._deps(eng, reads, writes)
        c = self.cnt[eng] + 1
        if inc:
            self.cnt[eng] = c
        self.stream[eng].append((waits, fn, eng if inc else None))
        for r in reads:
            self.readers.setdefault(r, {})[eng] = c
        for w in writes:
            self.lastw[w] = (eng, c)
            self.readers[w] = {}

    def dma(self, q, fn, reads=(), writes=()):
        half = self.ndma // 2
        base = 0 if q == "sp" else half
        slot = base + self.dma_next[q]
        self.dma_next[q] = (self.dma_next[q] + 1) % half
        key = ("d", slot)
        reads, writes = self._norm(reads, writes)
        reads = list(reads) + self.bar
        waits = self._deps(q, reads, writes)
        prev = self.cnt[key]
        if prev > 0 and self.waited[q].get(key, 0) < prev:
            self.waited[q][key] = prev
            waits.append((key, prev))
        c = prev + 16
        self.cnt[key] = c
        self.stream[q].append((waits, fn, key))
        for r in reads:
            self.readers.setdefault(r, {})[key] = c
        for w in writes:
            self.lastw[w] = (key, c)
            self.readers[w] = {}

    def emit(self, name, eng):
        for waits, fn, inckey in self.stream[name]:
            for k, c in waits:
                eng.wait_ge(self.sem[k], c)
            ins = fn(eng)
            if inckey is not None:
                ins.then_inc(self.sem[inckey], 16 if isinstance(inckey, tuple) else 1)


def build_nc(stage=99, sub=99, nj=4):
    nc = bass.Bass("TRN2", target_bir_lowering=False)

    def din(n, s):
        return nc.dram_tensor(n, s, F32, kind="ExternalInput").ap()
    xc = din("xc", [2048, 2048])
    xcT = din("xcT", [2048, 2048])
    pT = din("pT", [256, 1024])
    w_in = din("w_in", [2048, 7168])
    w_out = din("w_out", [2048, 2048])
    w_gate = din("w_gate", [2048, 2048])
    w_pp = din("w_pp", [256, 2048])
    cst_d = din("cst", [128, NCST])
    pn_rep = din("pn_rep", [128, 2048])
    y = nc.dram_tensor("y", [1024, 2048], F32, kind="ExternalOutput").ap()

    cur = [(nc.sbuf_base + 63) // 64 * 64]
    top = nc.sbuf_top

    def nbytes(shape, dt):
        n = 1
        for s in shape[1:]:
            n *= s
        return n * (4 if dt == F32 else 2)

    def alloc(name, shape, dt, at=None):
        off = cur[0] if at is None else at
        t = nc.alloc_sbuf_tensor_at(name, shape, dt, offset=off)
        if at is None:
            cur[0] = (off + nbytes(shape, dt) + 63) // 64 * 64
            assert cur[0] <= top, (name, cur[0], top)
        return t

    R1 = cur[0]
    xgT = alloc("xgT", [128, 16, 2048], BF16)
    h1 = alloc("h1", [128, 8, 2048], F32, at=R1)
    wbuf = [alloc("wbuf0", [128, 16, 512], BF16), alloc("wbuf1", [128, 16, 512], BF16)]
    RA = cur[0]
    mixedT = alloc("mixedT", [128, 16, 1024], BF16)
    hnT = alloc("hnT", [128, 16, 1024], BF16, at=RA)
    xTs = [alloc("xT%d" % i, [128, 2048], F32, at=RA + 8192 * i) for i in range(4)]
    cst = alloc("cst", [128, NCST], F32)
    sm = alloc("sm", [128, 320], F32)
    identb = alloc("identb", [128, 128], BF16)
    mhalf = alloc("mhalf", [128, 16], F32)
    sge = alloc("sge", [128, 512], F32)
    junkf = alloc("junkf", [128, 128], F32)
    junks = [alloc("junk%d" % i, [128, 128], F32) for i in range(4)]
    mtok = [alloc("mtok%d" % i, [128, 128], BF16) for i in range(4)]
    PT = [alloc("PT%d" % i, [128, 512], BF16) for i in range(4)]
    RB = cur[0]
    rQT = alloc("rQT", [128, 1024], BF16)
    rQsT = alloc("rQsT", [128, 1024], BF16)
    rKT = alloc("rKT", [128, 2048], BF16)
    rKz = alloc("rKz", [128, 16, 128], BF16)
    rV = alloc("rV", [128, 16, 256], BF16)
    rsg = alloc("rsg", [128, 8, 512], BF16)
    zs = [alloc("zs0", [128, 256], F32), alloc("zs1", [128, 256], F32)]
    rE = alloc("rE", [128, 256], F32)
    rF = alloc("rF", [128, 256], F32)
    qkr = [alloc("qkr%d" % i, [128, 256], BF16) for i in range(3)]
    AT = [alloc("AT0", [128, 128], BF16), alloc("AT1", [128, 128], BF16)]
    Rst2 = [alloc("Rst_%d" % i, [128, 128], F32) for i in range(2)]
    Rb2 = [[alloc("Rb%d_%d" % (i, k), [128, 128], BF16) for k in range(2)] for i in range(2)]
    DQ0 = cur[0]
    dQT = [alloc("dQT0", [128, 1024], BF16), None]
    dKT = [alloc("dKT0", [128, 2048], BF16), None]
    Vaug = [alloc("Vaug0", [128, 16, 132], BF16), None]
    dsg = [alloc("dsg0", [128, 8, 128], BF16), None]
    zsq = [alloc("zsq0", [128, 256], F32), alloc("zsq1", [128, 256], F32)]
    qkn = [alloc("qkn%d" % i, [128, 256], BF16) for i in range(3)]
    d1 = [alloc("d1_0", [128, 128], F32), alloc("d1_1", [128, 128], F32)]
    assert cur[0] - DQ0 >= 16384
    xs = [alloc("xs0", [128, 2048], F32, at=DQ0), alloc("xs1", [128, 2048], F32, at=DQ0 + 8192)]
    _save = cur[0]
    cur[0] = RB
    dQT[1] = alloc("dQT1", [128, 1024], BF16)
    dKT[1] = alloc("dKT1", [128, 2048], BF16)
    Vaug[1] = alloc("Vaug1", [128, 16, 132], BF16)
    dsg[1] = alloc("dsg1", [128, 8, 128], BF16)
    Osb = alloc("Osb", [128, 3, 396], F32)
    dd4 = alloc("dd4", [128, 4, 128], F32)
    zsb = [alloc("zsb%d" % i, [128, 512], F32) for i in range(3)]
    assert cur[0] <= _save
    cur[0] = _save
    ysb = alloc("ysb", [128, 512], F32)
    RB_end = cur[0]
    cur[0] = RB
    xr = [alloc("xr%d" % i, [128, 512], F32) for i in range(2)]
    sig = [alloc("sig%d" % i, [128, 512], F32) for i in range(2)]
    ot = [alloc("ot%d" % i, [128, 512], F32) for i in range(2)]
    pTb = alloc("pTb", [128, 2, 1024], BF16)
    wpp = alloc("wpp", [128, 2, 2048], BF16)
    hnb = [alloc("hnb%d" % i, [128, 2048], BF16) for i in range(2)]
    gtab = alloc("gtab", [128, 2048], F32)
    assert cur[0] <= RB_end or cur[0] <= top
    cur[0] = max(cur[0], RB_end)

    def C(name, a=None, b=None):
        o, w = _CST[name]
        if a is None:
            return cst[:, o:o + w]
        return cst[:, o + a:o + b]

    SSQX, MSX, SQX, RSTD, EPSQ = 0, 16, 32, 48, 64
    EPSC, LAMS, QN2S, SUBS, KN2 = 80, 81, 90, 91, 92
    HSS, HSQ, HR = 96, 104, 112
    FBASE = 128
    NRSTD = 260

    def smc(i, n=1):
        return sm[:, i:i + n]

    with contextlib.ExitStack() as es:
        eng_sems = {e: es.enter_context(nc.semaphore("s_" + e)) for e in ["pe", "act", "dve", "pool"]}
        dma_sems = [es.enter_context(nc.semaphore("s_d%d" % i)) for i in range(24)]
        S = Sched(eng_sems, dma_sems)
        zps = [es.enter_context(nc.psum_tensor("zps%d" % i, [128, 512], F32)) for i in range(2)]
        tps = es.enter_context(nc.psum_tensor("tps", [128, 1024], BF16))
        pA = es.enter_context(nc.psum_tensor("pA", [128, 512], F32))
        pB = es.enter_context(nc.psum_tensor("pB", [128, 512], F32))
        tpsB = pB.bitcast(BF16)
        pO = [es.enter_context(nc.psum_tensor("pO%d" % i, [128, 512], F32)) for i in range(3)]

        jk_i = [0]

        def ACT(out, in_, func, reads, writes, scale=None, bias=None, accum=None):
            if out is JUNK:
                jk_i[0] += 1
                out = junks[jk_i[0] % 4][:]
                writes = [w for w in writes if w != "junkf"] + [("junk", jk_i[0] % 4)]
            kw = {}
            if scale is not None:
                kw["scale"] = scale
            if bias is not None:
                kw["bias"] = bias
            if accum is not None:
                kw["accum_out"] = accum
            S.op("act", lambda e: e.activation(out=out, in_=in_, func=func, **kw), reads, writes)

        def TS(out, in0, s1, s2, op0, op1, reads, writes, eng="dve"):
            if op1 is None:
                S.op(eng, lambda e: e.tensor_scalar(out=out, in0=in0, scalar1=s1, scalar2=None, op0=op0), reads, writes)
            else:
                S.op(eng, lambda e: e.tensor_scalar(out=out, in0=in0, scalar1=s1, scalar2=s2, op0=op0, op1=op1), reads, writes)

        def TT(out, in0, in1, op, reads, writes, eng="dve"):
            S.op(eng, lambda e: e.tensor_tensor(out=out, in0=in0, in1=in1, op=op), reads, writes)

        def STT(out, in0, scalar, in1, op0, op1, reads, writes):
            S.op("dve", lambda e: e.scalar_tensor_tensor(out=out, in0=in0, scalar=scalar, in1=in1, op0=op0, op1=op1), reads, writes)

        def RED(out, in_, reads, writes):
            S.op("dve", lambda e: e.tensor_reduce(out=out, in_=in_, axis=AX.X, op=ALU.add), reads, writes)

        def RECIP(out, in_, reads, writes):
            S.op("dve", lambda e: e.reciprocal(out=out, in_=in_), reads, writes)

        def RSQ(out, in_, n, reads, writes):
            S.op("pool", lambda e: e.tensor_tensor(out=out, in0=in_, in1=mhalf[:, 0:n], op=ALU.pow),
                 list(reads) + ["mhalf"], writes)

        def SILU_SB(out_bf, y_ap, ykey, width, writes):
            e_ap = sge[:, 0:width]
            ACT(e_ap, y_ap, AF.Exp, [ykey], ["sge"], scale=-1.0)
            TS(e_ap, e_ap, 1.0, None, ALU.add, None, ["sge"], ["sge"])
            RECIP(e_ap, e_ap, ["sge"], ["sge"])
            TT(out_bf, y_ap, e_ap, ALU.mult, [ykey, "sge"], writes)

        def MM(out, lhsT, rhs, start, stop, reads, writes, inc, sgc=False):
            if sgc:
                S.op("pe", lambda e: e.matmul(out, lhsT, rhs, start=start, stop=stop, skip_group_check=True),
                     reads, writes, inc=inc)
            else:
                S.op("pe", lambda e: e.matmul(out, lhsT, rhs, start=start, stop=stop), reads, writes, inc=inc)

        def TR(out, in_, reads, writes):
            S.op("pe", lambda e: e.transpose(out, in_, identb[:]), list(reads) + ["identb"], writes)

        def DMA(q, out, in_, reads, writes):
            S.dma(q, lambda e: e.dma_start(out=out, in_=in_), reads, writes)

        def MEMSET(eng, ap, val, writes):
            S.op(eng, lambda e: e.memset(ap, val), [], writes)

        tps_i = [0]

        def tslot():
            i = tps_i[0] % 8
            tps_i[0] += 1
            return i, tps[:, i * 128:(i + 1) * 128], ("tps", i)

        f_i = [0]

        def fcols(n):
            i = f_i[0] % 32
            f_i[0] += 1
            return sm[:, FBASE + i * 4:FBASE + i * 4 + n], ("fs", i)

        m_i = [0]

        groups = []
        col = 0
        GA, G3, GD = [], [], []
        for j in range(4):
            GA.append(len(groups)); groups.append((w_in[:, col:col + 512], 512)); col += 512
            if j % 2 == 0:
                G3.append(len(groups)); groups.append((w_in[:, col:col + 512], 512)); col += 512
        for h in range(8):
            GD.append(len(groups)); groups.append((w_in[:, col:col + 512], 512)); col += 512
        assert col == 7168
        GO = []
        for n in range(4):
            GO.append(len(groups)); groups.append((w_out[:, n * 512:(n + 1) * 512], 512))
        GG = []
        for n in range(4):
            GG.append(len(groups)); groups.append((w_gate[:, n * 512:(n + 1) * 512], 512))
        loaded = [0]

        def load_group(g):
            ap, ncols = groups[g]
            slot = g % 2
            apr = ap.rearrange("(kc p) n -> p kc n", p=128)
            for part in range(4):
                DMA("pool", wbuf[slot][:, part * 4:part * 4 + 4, 0:ncols], apr[:, part * 4:part * 4 + 4, :], [],
                    [("wbuf", slot, part)])

        def ensure_loaded(g):
            while loaded[0] <= min(g + 1, len(groups) - 1):
                load_group(loaded[0])
                loaded[0] += 1

        DMA("sp", cst[:], cst_d, [], ["cst"])
        ensure_loaded(0)
        MEMSET("dve", smc(EPSC), EPS, [("sm", "eps")])
        TT(identb[:], C("ident"), C("ident"), ALU.max, ["cst"], ["identb"])
        TT(junkf[:, 0:64], C("lq1"), C("lk1"), ALU.mult, ["cst"], ["junkf"])
        RED(smc(LAMS + 0), junkf[:, 0:64], ["junkf"], [("sm", "l0")])
        TT(junkf[:, 64:128], C("lq2"), C("lk2"), ALU.mult, ["cst"], ["junkf2"])
        RED(smc(LAMS + 1), junkf[:, 64:128], ["junkf2"], [("sm", "l1")])
        ACT(smc(LAMS + 2, 2), smc(LAMS + 0, 2), AF.Exp, [("sm", "l0"), ("sm", "l1")], [("sm", "l2")])
        TT(smc(LAMS + 4), smc(LAMS + 2), smc(LAMS + 3), ALU.subtract, [("sm", "l2")], [("sm", "l4")])
        TS(smc(LAMS + 5), smc(LAMS + 4), -1.0, -LAM_INIT, ALU.mult, ALU.add, [("sm", "l4")], [("sm", "neglam")])
        NEGLAM = smc(LAMS + 5)
        TS(smc(QN2S), C("qn2"), C("kn2"), 0.125, ALU.mult, ALU.mult, ["cst"], [("sm", "qn2s")])
        TS(smc(SUBS), C("subln"), 1.0 - LAM_INIT, None, ALU.mult, None, ["cst"], [("sm", "subs")])

        MEMSET("pool", mhalf[:], -0.5, ["mhalf"])
        for kc in range(16):
            DMA("sp", xTs[kc % 4][:], xcT[kc * 128:(kc + 1) * 128, :], [], [("xT", kc % 4)])
            if kc % 2 == 0:
                TS(xgT[:, kc, :], xTs[kc % 4][:], C("an_g", kc, kc + 1), None, ALU.mult, None,
                   [("xT", kc % 4), "cst"], [("R1k", kc)])
            else:
                ACT(xgT[:, kc, :], xTs[kc % 4][:], AF.Copy, [("xT", kc % 4), "cst"], [("R1k", kc)],
                    scale=C("an_g", kc, kc + 1))
        for t in range(2):
            DMA("sp", xs[t % 2][:], xc[t * 128:(t + 1) * 128, :], [], [("xs", t % 2)])

        def stats(t):
            ACT(xs[t % 2][:], xs[t % 2][:], AF.Square, [("xs", t % 2)], [("xs", t % 2), ("ssqx", t)], accum=smc(SSQX + t))
            TS(smc(MSX + t), smc(SSQX + t), 1.0 / 2048, EPS, ALU.mult, ALU.add, [("ssqx", t)], [("msx", t)])
            RSQ(smc(RSTD + t), smc(MSX + t), 1, [("msx", t)], [("rstd", t)])
            if t + 2 < 16:
                DMA("sp", xs[t % 2][:], xc[(t + 2) * 128:(t + 3) * 128, :], [], [("xs", t % 2)])

        def rstd(t):
            return smc(RSTD + t)

        zi = [0]

        def proj_part(t, g, c0, c1, i, k0, k1):
            wb = wbuf[g % 2]
            for kc in range(k0, k1):
                MM(zps[i][:, c0:c1], xgT[:, kc, t * 128:(t + 1) * 128], wb[:, kc, c0:c1], kc == 0, kc == 15,
                   [("wbuf", g % 2, kc // 4), "R1", ("R1k", kc)], [("zps", i)], kc == 15)

        def proj(t, g, c0, c1, mid=None):
            i = zi[0] % 2
            zi[0] += 1
            proj_part(t, g, c0, c1, i, 0, 8)
            if mid is not None:
                mid()
            proj_part(t, g, c0, c1, i, 8, 16)
            return zps[i], ("zps", i)

        def finalize(o_ap, o_key, sg_ap, sg_key, gain_ap, gain_key, fchunk, ot_idx):
            fc, fk = fcols(3)
            ACT(junkf[:], o_ap, AF.Square, [o_key], ["junkf", fk], accum=fc[:, 0:1])
            ACT(fc[:, 1:2], fc[:, 0:1], AF.Sqrt, [fk, ("sm", "eps")], [fk], scale=1.0 / 128, bias=smc(EPSC))
            RECIP(fc[:, 2:3], fc[:, 1:2], [fk], [fk])
            mi = m_i[0] % 2
            m_i[0] += 1
            STT(mtok[mi][:], o_ap, fc[:, 2:3], sg_ap, ALU.mult, ALU.mult, [o_key, fk, sg_key], [("mtok", mi)])
            ti, tap, tk = tslot()
            TR(tap, mtok[mi][:], [("mtok", mi)], [tk])
            ACT(mixedT[:, fchunk, ot_idx * 128:(ot_idx + 1) * 128], tap, AF.Copy, [tk, gain_key],
                [("mixT", fchunk)], scale=gain_ap)

        marks = []

        def phase_barrier():
            marks.append(len(S.stream["pe"]))
            S.barrier({
                "pe": (lambda e: e.transpose(tps[:, 0:128], identb[:], identb[:]), [("tps", 0)]),
                "act": (lambda e: e.activation(out=sm[:, 300:301], in_=sm[:, 301:302], func=AF.Copy), []),
                "dve": (lambda e: e.memset(sm[:, 302:303], 0.0), []),
                "pool": (lambda e: e.memset(sm[:, 303:304], 0.0), []),
            })

        MEMSET("dve", sm[:, 300:304], 0.0, ["smbar"])
        o_c, o_s, o_z, o_x, o_d = (_CST[n][0] for n in ("cos", "sin", "zeta", "XiT", "DT"))

        def COPY(out, in_, reads, writes):
            S.op("dve", lambda e: e.tensor_copy(out=out, in_=in_), reads, writes)

        def ret_pair(j, carry):
            Rst = Rst2[j % 2]
            Rb = Rb2[j % 2]
            RK = "Rst%d" % (j % 2)

            def rbk(k):
                return ("Rb", j % 2, k)
            g = GA[j]
            ensure_loaded(g)
            g3 = G3[j // 2] if j % 2 == 0 else None
            MEMSET("dve", Rst[:], 0.0, [RK])
            MEMSET("dve", Rb[0][:], 0.0, [rbk(0)])

            st_r = {}

            def rA(t):
                own = t >= 8
                c0 = 0 if own else 128
                G = 4 if own else 2
                if j == 0:
                    stats(t)
                ps, pk = proj(t, g, c0, 512, mid=lambda: small_ops(t))
                if own and g3 is not None:
                    ps3, pk3 = proj(t, g3, 0, 512)
                z = zs[t % 2]
                ACT(z[:, c0:256], ps[:, c0:256], AF.Copy, [pk, ("rstd", t)], [("zs", t % 2)], scale=rstd(t))
                ACT(rV[:, t, :], ps[:, 256:512], AF.Copy, [pk, ("rstd", t)], [("rV", t)], scale=rstd(t))
                if own and g3 is not None:
                    ACT(rsg[:, t - 8, :], ps3[:, :], AF.Silu, [pk3, ("rstd", t)], [("rsg", t - 8)], scale=rstd(t))
                zv = z[:, c0:256].rearrange("p (g h d) -> p g h d", g=G, h=2)
                Ev = rE[:, c0:256].rearrange("p (g h d) -> p g h d", g=G, h=2)
                Fv = rF[:, c0:256].rearrange("p (g h d) -> p g h d", g=G, h=2)
                qv = qkr[(t + j) % 3][:, c0:256].rearrange("p (g h d) -> p g h d", g=G, h=2)
                cosb = cst[:, o_c + t * 32:o_c + (t + 1) * 32].unsqueeze(1).unsqueeze(1).broadcast_to([128, G, 2, 32])
                sinb = cst[:, o_s + t * 32:o_s + (t + 1) * 32].unsqueeze(1).unsqueeze(1).broadcast_to([128, G, 2, 32])
                TT(Ev, zv, cosb, ALU.mult, [("zs", t % 2), "cst"], ["rE"], eng="pool")
                TT(Fv, zv, sinb, ALU.mult, [("zs", t % 2), "cst"], ["rF"], eng="pool")
                TT(qv[:, :, 0, :], Ev[:, :, 0, :], Fv[:, :, 1, :], ALU.subtract, ["rE", "rF"], [("qkr", (t + j) % 3)])
                TT(qv[:, :, 1, :], Fv[:, :, 0, :], Ev[:, :, 1, :], ALU.add, ["rE", "rF"], [("qkr", (t + j) % 3)])
                if t < 15:
                    zb = cst[:, o_z + 2 * j:o_z + 2 * j + 2].unsqueeze(2).broadcast_to([128, 2, 64])
                    TT(rKz[:, t, :].rearrange("p (a d) -> p a d", a=2),
                       qkr[(t + j) % 3][:, 128:256].rearrange("p (a d) -> p a d", a=2), zb, ALU.mult,
                       [("qkr", (t + j) % 3), "cst"], [("rKz", t)], eng="pool")

            def rC(t):
                own = t >= 8
                if not own:
                    return
                TR(tps[:, 0:128], qkr[(t + j) % 3][:, 128:256], [("qkr", (t + j) % 3)], [("tps", 0)])
                TR(tps[:, 128:256], qkr[(t + j) % 3][:, 0:128], [("qkr", (t + j) % 3)], [("tps", 0)])
                ACT(rKT[:, (t - 8) * 128:(t - 7) * 128], tps[:, 0:128], AF.Copy, [("tps", 0)], [("rKT", t)])
                ACT(rQT[:, (t - 8) * 128:(t - 7) * 128], tps[:, 128:256], AF.Copy, [("tps", 0)], [("rQT", t)])
                TT(rQsT[:, (t - 8) * 128:(t - 7) * 128], rQT[:, (t - 8) * 128:(t - 7) * 128],
                   cst[:, o_x + j * 128:o_x + (j + 1) * 128], ALU.mult, [("rQT", t), "cst"], [("rQsT", t)], eng="pool")

            def rD1(t):
                own = t >= 8
                for hl in range(2):
                    P0 = hl * 64
                    if own:
                        MM((pA, pB)[hl][:, 0:128], rKT[P0:P0 + 64, (t - 8) * 128:(t - 7) * 128],
                           rQT[P0:P0 + 64, (t - 8) * 128:(t - 7) * 128], True, True,
                           [("rKT", t), ("rQT", t)], [(("pA", 0), ("pB", 0))[hl]], True)
                    if t < 15:
                        MM(pO[0][P0:P0 + 64, 0:128], rKz[:, t, hl * 64:(hl + 1) * 64], rV[:, t, hl * 128:(hl + 1) * 128],
                           True, True, [("rKz", t), ("rV", t)], [("kv", hl)], True)
                if own:
                    for hl in range(2):
                        h = 2 * j + hl
                        TT(AT[hl][:], (pA, pB)[hl][:, 0:128], cst[:, o_d + h * 128:o_d + (h + 1) * 128], ALU.mult,
                           [(("pA", 0), ("pB", 0))[hl], "cst"], [("AT", hl)])
                if t < 15:
                    STT(Rb[(t + 1) % 2][:], Rst[:], C("g128", j, j + 1), pO[0][:, 0:128], ALU.mult, ALU.add,
                        [RK, "cst", ("kv", 0), ("kv", 1)], [rbk((t + 1) % 2)])
                    STT(Rst[:], Rst[:], C("g128", j, j + 1), pO[0][:, 0:128], ALU.mult, ALU.add,
                        [RK, "cst", ("kv", 0), ("kv", 1)], [RK])

            def rD2(t):
                fa, fka = fcols(4)
                fb, fkb = fcols(4)
                for hl in range(2):
                    P0 = hl * 64
                    o_ap = pO[1 + hl][:, 0:128]
                    okey = ("Ob", 1 + hl)
                    MM(o_ap, AT[hl][:], rV[:, t, hl * 128:(hl + 1) * 128], True, False,
                       [("AT", hl), ("rV", t)], [okey], False)
                    MM(o_ap, rQsT[P0:P0 + 64, (t - 8) * 128:(t - 7) * 128], Rb[t % 2][P0:P0 + 64, :], False, True,
                       [("rQsT", t), rbk(t % 2)], [okey], True)
                for hl in range(2):
                    ACT(JUNK, pO[1 + hl][:, 0:128], AF.Square, [("Ob", 1 + hl)], ["junkf", fka], accum=fa[:, hl:hl + 1])
                TS(fb[:, 0:2], fa[:, 0:2], 1.0 / 128, EPS, ALU.mult, ALU.add, [fka], [fkb])
                RSQ(fa[:, 2:4], fb[:, 0:2], 2, [fkb], [fka])
                st_r[t] = (fa, fka)

            def rD2b(t):
                fa, fka = st_r[t]
                for hl in range(2):
                    h = 2 * j + hl
                    STT(mtok[hl][:], pO[1 + hl][:, 0:128], fa[:, 2 + hl:3 + hl],
                        rsg[:, t - 8, (h % 4) * 128:(h % 4 + 1) * 128], ALU.mult, ALU.mult,
                        [("Ob", 1 + hl), fka, ("rsg", t - 8)], [("mtok", hl)])

            def rEe(t):
                for hl in range(2):
                    TR(tps[:, hl * 128:(hl + 1) * 128], mtok[hl][:], [("mtok", hl)], [("tps", 0)])
                for hl in range(2):
                    h = 2 * j + hl
                    ACT(mixedT[:, h, (t - 8) * 128:(t - 7) * 128], tps[:, hl * 128:(hl + 1) * 128], AF.Copy,
                        [("tps", 0), "cst"], [("mixT", h, t - 8)], scale=C("retgn", h, h + 1))

            def small_ops(s_):
                if s_ < len(carry):
                    carry[s_]()
                if 8 <= s_ - 5 < 16:
                    rEe(s_ - 5)
                if 8 <= s_ - 4 < 16:
                    rD2(s_ - 4)
                    rD2b(s_ - 4)
                if 0 <= s_ - 3 < 16:
                    rD1(s_ - 3)
                if 0 <= s_ - 2 < 16:
                    rC(s_ - 2)

            for s_ in range(16):
                rA(s_)
            return [(lambda s_=s_: small_ops(s_)) for s_ in range(16, 16 + 5)]


        carry = []
        for j in (range(nj) if stage >= 1 else []):
            carry = ret_pair(j, carry)
        for f in carry:
            f()

        phase_barrier()
        OREG = [(0, 0), (0, 132), (0, 264), (1, 0), (1, 132), (1, 264), (2, 0), (2, 132)]
        if stage >= 2:
            TT(Vaug[0][:, :, 128:129], C("flag").unsqueeze(2), C("flag").unsqueeze(2), ALU.max, ["cst"], [("vflag", 0)])
            TT(Vaug[1][:, :, 128:129], C("flag").unsqueeze(2), C("flag").unsqueeze(2), ALU.max, ["cst"], [("vflag", 1)])

        def d_proj_thunks(h):
            g = GD[h]
            bs = h % 2
            st = {}

            def tile_thunks(t):
                own = t >= 8
                c0 = 0 if own else 128
                G = 4 if own else 2
                zb_ = zsb[t % 3]
                zk = ("zsb", t % 3)
                zq = zsq[t % 2]

                def a0():
                    if t == 0:
                        ensure_loaded(g)
                    i = 0
                    st[("zi", t)] = i
                    proj_part(t, g, c0, 512 if own else 384, i, 0, 8)

                def a0b():
                    i = st[("zi", t)]
                    proj_part(t, g, c0, 512 if own else 384, i, 8, 16)
                    st[t] = (zps[i], ("zps", i))

                def a1():
                    ps, pk = st[t]
                    if own:
                        ACT(zb_[:, 0:512], ps[:, 0:512], AF.Copy, [pk, ("rstd", t)], [zk], scale=rstd(t))
                    else:
                        ACT(zb_[:, 128:384], ps[:, 128:384], AF.Copy, [pk, ("rstd", t)], [zk], scale=rstd(t))
                    TS(Vaug[bs][:, t, 0:128], zb_[:, 256:384], 1.0, 1.0, ALU.mult, ALU.mult, [zk], [("Vaug", bs, t)],
                       eng="pool")

                def a2():
                    TT(zq[:, c0:256], zb_[:, c0:256], zb_[:, c0:256], ALU.mult, [zk], [("zsq", t % 2)], eng="pool")
                    fc, fk = fcols(4)
                    fc2, fk2 = fcols(4)
                    st[("fc", t)] = (fc, fk)
                    RED(fc[:, 0:G], zq[:, c0:256].rearrange("p (g d) -> p g d", g=G), [("zsq", t % 2)], [fk])
                    TS(fc2[:, 0:G], fc[:, 0:G], 1.0 / 64, EPS, ALU.mult, ALU.add, [fk], [fk2])
                    RSQ(fc[:, 0:G], fc2[:, 0:G], G, [fk2], [fk])

                def a3():
                    fc, fk = st[("fc", t)]
                    if own:
                        ACT(sge[:, 0:128], zb_[:, 384:512], AF.Exp, [zk], ["sge"], scale=-1.0)
                    TT(qkn[t % 3][:, c0:256].rearrange("p (g d) -> p g d", g=G),
                       zb_[:, c0:256].rearrange("p (g d) -> p g d", g=G),
                       fc[:, 0:G].unsqueeze(2).broadcast_to([128, G, 64]), ALU.mult, [zk, fk], [("qkn", t % 3)])
                    if own:
                        TS(sge[:, 0:128], sge[:, 0:128], 1.0, None, ALU.add, None, ["sge"], ["sge"])

                def a4():
                    if own:
                        RECIP(sge[:, 0:128], sge[:, 0:128], ["sge"], ["sge"])
                        TT(dsg[bs][:, t - 8, :], zb_[:, 384:512], sge[:, 0:128], ALU.mult, [zk, "sge"], [("dsg", bs, t - 8)])
                return [a0, a0b, a1, a2, a3, a4] if own else [a0, a0b, a1, a2, a3]

            def dC(t):
                own = t >= 8
                TR(tps[:, 0:128], qkn[t % 3][:, 128:256], [("qkn", t % 3)], [("tps", 0)])
                if own:
                    TR(tps[:, 128:256], qkn[t % 3][:, 0:128], [("qkn", t % 3)], [("tps", 0)])
                ACT(dKT[bs][:, t * 128:(t + 1) * 128], tps[:, 0:128], AF.Copy, [("tps", 0)], [("dKT", bs, t)])
                if own:
                    ACT(dQT[bs][:, (t - 8) * 128:(t - 7) * 128], tps[:, 128:256], AF.Copy, [("tps", 0), ("sm", "qn2s")],
                        [("dQT", bs, t)], scale=smc(QN2S))

            th = []
            for s_ in range(18):
                if s_ < 16:
                    tt_ = tile_thunks(s_)
                    th.append(tt_[0])
                if 0 <= s_ - 2 < 16:
                    th.append(lambda t=s_ - 2: dC(t))
                if s_ < 16:
                    th.extend(tt_[1:])
            return th

        def d_attn_thunks(h):
            bs = h % 2
            steps = [(qt, c, kb) for qt in range(2) for c in range(2) for kb in range(8 + 4 * qt + 4)]

            def oreg(c, qb):
                b_, o_ = OREG[c * 4 + qb]
                return pO[b_][:, o_:o_ + 129], ("Ob", b_)

            def emitS(i):
                qt, c, kb = steps[i]
                t0 = 8 + 4 * qt
                qb0 = max(kb - t0, 0)
                qlo = qb0 * 128
                P0 = c * 64
                si = i % 3
                pi = i % 4
                sps = (pA, pB, zps[1])[si]
                MM(sps[:, qlo:512], dKT[bs][P0:P0 + 64, kb * 128:(kb + 1) * 128],
                   dQT[bs][P0:P0 + 64, qt * 512 + qlo:(qt + 1) * 512], True, True,
                   [("dKT", bs, kb)] + [("dQT", bs, t0 + q) for q in range(qb0, 4)], [("sps", si)], True)
                ACT(PT[pi][:, qlo:512], sps[:, qlo:512], AF.Exp, [("sps", si)], [("PT", pi)])
                if kb >= t0:
                    ACT(PT[pi][64:128, qlo:qlo + 64], PT[pi][64:128, qlo:qlo + 64], AF.Copy, [("PT", pi)], [("PT", pi)],
                        scale=0.0)

            def emitAV(i):
                qt, c, kb = steps[i]
                t0 = 8 + 4 * qt
                qb0 = max(kb - t0, 0)
                pi = i % 4
                for qb in range(qb0, 4):
                    oa, ok = oreg(c, qb)
                    last = (kb == t0 + qb)
                    first_in_bank = (c * 4 + qb) in (0, 3, 6)
                    MM(oa, PT[pi][:, qb * 128:(qb + 1) * 128], Vaug[bs][:, kb, 0:129], kb == 0 and first_in_bank, last,
                       [("PT", pi), ("Vaug", bs, kb), ("vflag", bs)], [ok], last or qb == 3, sgc=True)

            def fin_thunks(qt):
                t0 = 8 + 4 * qt
                fa, fka = sm[:, 276 + 8 * qt:280 + 8 * qt], ("finA", qt)
                fb, fkb = sm[:, 280 + 8 * qt:284 + 8 * qt], ("finB", qt)
                out = []

                def f0():
                    for b_ in range(3):
                        nr = 3 if b_ < 2 else 2
                        COPY(Osb[:, b_, 0:nr * 132].rearrange("p (r c) -> p r c", c=132)[:, :, 0:129],
                             pO[b_][:, 0:nr * 132].rearrange("p (r c) -> p r c", c=132)[:, :, 0:129],
                             [("Ob", b_)], [("Osb", b_)])
                    RECIP(sm[:, 310:318], Osb[:].rearrange("p b c -> p (b c)")[:, 128:128 + 132 * 8:132],
                          [("Osb", 0), ("Osb", 1), ("Osb", 2)], ["rl8"])
                    TS(sm[:, 314:318], sm[:, 314:318], NEGLAM, None, ALU.mult, None, ["rl8", ("sm", "neglam")], ["rl8"])
                out.append(f0)
                for qb in range(4):
                    def f1(qb=qb):
                        b0, of0 = OREG[qb]
                        b1, of1 = OREG[4 + qb]
                        di = qb % 2
                        TS(d1[di][:], Osb[:, b0, of0:of0 + 128], sm[:, 310 + qb:311 + qb], None, ALU.mult, None,
                           [("Osb", b0), "rl8"], [("d1", di)])
                        STT(dd4[:, qb, :], Osb[:, b1, of1:of1 + 128], sm[:, 314 + qb:315 + qb], d1[di][:], ALU.mult, ALU.add,
                            [("Osb", b1), "rl8", ("d1", di)], [("dd4", qb)])
                        jk_i[0] += 1
                        jq = jk_i[0] % 4
                        TT(junks[jq][:], dd4[:, qb, :], dd4[:, qb, :], ALU.mult, [("dd4", qb)], [("junk", jq)])
                        RED(fa[:, qb:qb + 1], junks[jq][:], [("junk", jq)], [fka])
                    out.append(f1)

                def f2():
                    TS(fb[:, 0:4], fa[:, 0:4], 1.0 / 128, EPS, ALU.mult, ALU.add, [fka], [fkb])
                    RSQ(fa[:, 0:4], fb[:, 0:4], 4, [fkb], [fka])
                out.append(f2)
                for qb in range(4):
                    def f3a(qb=qb):
                        tq = t0 + qb
                        STT(mtok[qb][:], dd4[:, qb, :], fa[:, qb:qb + 1], dsg[bs][:, tq - 8, :], ALU.mult, ALU.mult,
                            [("dd4", qb), fka, ("dsg", bs, tq - 8)], [("mtok", qb)])
                    out.append(f3a)
                for qb in range(4):
                    def f3b(qb=qb):
                        tq = t0 + qb
                        mi = qb % 2
                        TR(tps[:, mi * 128:(mi + 1) * 128], mtok[qb][:], [("mtok", qb)], [("tps", 0)])
                        ACT(mixedT[:, 8 + h, (tq - 8) * 128:(tq - 7) * 128], tps[:, mi * 128:(mi + 1) * 128], AF.Copy,
                            [("tps", 0), ("sm", "subs")], [("mixT", 8 + h, tq - 8)], scale=smc(SUBS))
                    out.append(f3b)
                return out

            th = [lambda: (emitS(0), emitS(1))]
            pos_end = {}
            for i in range(len(steps)):
                def f(i=i):
                    if i + 2 < len(steps):
                        emitS(i + 2)
                    emitAV(i)
                th.append(f)
                qt, c, kb = steps[i]
                if c == 1 and kb == 8 + 4 * qt + 3:
                    pos_end[qt] = len(th)
            fin0 = fin_thunks(0)
            fin1 = fin_thunks(1)
            p0 = pos_end[0]
            out = (th[:p0] + fin0[:6] + th[p0:p0 + 6] + fin0[6:10] + th[p0 + 6:p0 + 12] + fin0[10:]
                   + th[p0 + 12:] + fin1[:6])
            return out, fin1[6:]

        def merge(A, P, frac=0.82):
            if not P:
                return list(A)
            if not A:
                return list(P)
            out = []
            ip = 0
            span = max(1.0, frac * len(A))
            for ia, a in enumerate(A):
                out.append(a)
                while ip < len(P) and (ip + 1) * span <= (ia + 1) * len(P):
                    out.append(P[ip])
                    ip += 1
            out.extend(P[ip:])
            return out

        if stage >= 2:
            for f in d_proj_thunks(0):
                f()
            tail = []
            for h in range(8):
                A_, nt = d_attn_thunks(h)
                A_ = A_[:6] + tail[:4] + A_[6:12] + tail[4:] + A_[12:]
                tail = nt
                P_ = d_proj_thunks(h + 1) if h + 1 < 8 else []
                for f in merge(A_, P_):
                    f()
            for f in tail:
                f()

        phase_barrier()
        x_i = [0]
        if stage >= 4:
            DMA("pool", pTb[:], pT.rearrange("(kc p) n -> p kc n", p=128), [], ["pTb"])
            DMA("pool", wpp[:], w_pp.rearrange("(kc p) n -> p kc n", p=128), [], ["wpp"])
            DMA("sp", gtab[:], pn_rep, [], ["gtab"])

        def h_stats(t):
            ACT(hnb[t % 2][:], h1[:, t, :], AF.Square, [("h1", t), "R1"], [("hnb", t % 2), ("hss", t)], accum=smc(HSS + t))
            TS(smc(HSQ + t), smc(HSS + t), 1.0 / 2048, EPS, ALU.mult, ALU.add, [("hss", t)], [("hsq", t)])
            RSQ(smc(HR + t), smc(HSQ + t), 1, [("hsq", t)], [("hr", t)])

        def h_norm(t):
            STT(hnb[t % 2][:], h1[:, t, :], smc(HR + t), gtab[:], ALU.mult, ALU.mult,
                [("h1", t), "R1", ("hr", t), "gtab"], [("hnb", t % 2)])

        def h_tr(t):
            for half in range(2):
                tb, tk = ((tps, ("tps", 0)), (tpsB, ("pB", 0)))[half]
                for k in range(8):
                    TR(tb[:, k * 128:(k + 1) * 128], hnb[t % 2][:, (half * 8 + k) * 128:(half * 8 + k + 1) * 128],
                       [("hnb", t % 2)], [tk])
                ACT(hnT[:, half * 8:half * 8 + 8, t * 128:(t + 1) * 128],
                    tb[:, :].rearrange("p (k n) -> p k n", k=8), AF.Copy, [tk],
                    [("hnT", t)] + [("mixT", f, t) for f in range(16)])

        for n in (range(4) if stage >= 3 else []):
            g = GO[n]
            ensure_loaded(g)
            for t in range(8):
                xi = x_i[0] % 2
                x_i[0] += 1
                DMA("sp", xr[xi][:], xc[(8 + t) * 128:(9 + t) * 128, n * 512:(n + 1) * 512], [], [("xr", xi)])
                i = zi[0] % 2
                zi[0] += 1
                for fc_ in range(16):
                    MM(zps[i][:, :], mixedT[:, fc_, t * 128:(t + 1) * 128], wbuf[g % 2][:, fc_, :], fc_ == 0, fc_ == 15,
                       [("wbuf", g % 2, fc_ // 4), ("mixT", fc_, t)], [("zps", i)], fc_ == 15)
                    if fc_ == 7 and n == 3 and stage >= 4 and t >= 2:
                        h_tr(t - 2)
                TT(h1[:, t, n * 512:(n + 1) * 512], zps[i][:, :], xr[xi][:], ALU.add, [("zps", i), ("xr", xi)],
                   ["R1"] if n == 0 and t == 0 else [("h1", t)])
                if n == 3 and stage >= 4:
                    if t >= 1:
                        h_norm(t - 1)
                    h_stats(t)
        if stage >= 4:
            h_tr(6)
            h_norm(7)
            h_tr(7)
        yv = y.rearrange("(a p) n -> p a n", p=128)
        if stage == 0:
            DMA("sp", yv[:, 0, 0:320], sm[:], [("sm", "neglam")], [])
            DMA("pool", yv[:, 1, :], xgT[:, 0, :], ["R1"], [])
            DMA("pool", yv[:, 2, :], xgT[:, 15, :], ["R1"], [])
        if stage in (1, 2):
            phase_barrier()
            for a in range(8):
                DMA("pool", yv[:, a, :].rearrange("p (c n) -> p c n", c=2), mixedT[:, 2 * a:2 * a + 2, :], [], [])
        if stage == 3:
            phase_barrier()
            DMA("sp", yv, h1[:], [], [])
        pp_i = [0]
        for n in (range(4) if stage >= 4 else []):
            g = GG[n]
            ensure_loaded(g)
            for t in range(8):
                i = zi[0] % 2
                zi[0] += 1
                for fc_ in range(16):
                    MM(zps[i][:, :], hnT[:, fc_, t * 128:(t + 1) * 128], wbuf[g % 2][:, fc_, :], fc_ == 0, fc_ == 15,
                       [("wbuf", g % 2, fc_ // 4), ("hnT", t)], [("zps", i)], fc_ == 15)
                pi = pp_i[0] % 2
                pp_i[0] += 1
                pps = (pA, pB)[pi]
                for k2 in range(2):
                    MM(pps[:, :], pTb[:, k2, t * 128:(t + 1) * 128], wpp[:, k2, n * 512:(n + 1) * 512], k2 == 0, k2 == 1,
                       ["pTb", "wpp"], [("pps", pi)], k2 == 1)
                ACT(sig[pi][:], zps[i][:, :], AF.Sigmoid, [("zps", i)], [("sig", pi)])
                TT(ot[pi][:], pps[:, :], sig[pi][:], ALU.mult, [("pps", pi), ("sig", pi)], [("ot", pi)])
                TT(ot[pi][:], ot[pi][:], h1[:, t, n * 512:(n + 1) * 512], ALU.add, [("ot", pi), ("h1", t), "R1"], [("ot", pi)])
                DMA("sp", y[t * 128:(t + 1) * 128, n * 512:(n + 1) * 512], ot[pi][:], [("ot", pi)], [])

        marks.append(len(S.stream["pe"]))
        build_nc.marks = marks
        final_waits = [(k, c) for k, c in S.cnt.items() if isinstance(k, tuple) and c > 0]

        block = es.enter_context(nc.Block())

        @block.sync
        def _(e):
            S.emit("sp", e)
            for k, c in final_waits:
                e.wait_ge(S.sem[k], c)

        @block.scalar
        def _(e):
            S.emit("act", e)

        @block.vector
        def _(e):
            S.emit("dve", e)

        @block.gpsimd
        def _(e):
            S.emit("pool", e)

        @block.tensor
        def _(e):
            S.emit("pe", e)
    return nc


def _host_tables(s2, attn_norm, ple_norm, ret_gn, diff_qn, diff_kn, lq1, lk1, lq2, lk2, subln):
    cstv = np.zeros((128, NCST), np.float32)

    def put(name, arr):
        o, w = _CST[name]
        cstv[:, o:o + w] = np.asarray(arr, np.float32).reshape(128, w)
    put("an_g", attn_norm.reshape(16, 128).T)
    put("pn_g", ple_norm.reshape(16, 128).T)
    put("retgn", ret_gn.reshape(8, 128).T)
    put("subln", subln.reshape(128, 1))
    put("qn2", np.concatenate([diff_qn, diff_qn]).reshape(128, 1))
    put("kn2", np.concatenate([diff_kn, diff_kn]).reshape(128, 1))
    for n, v in (("lq1", lq1), ("lk1", lk1), ("lq2", lq2), ("lk2", lk2)):
        put(n, np.broadcast_to(v.reshape(1, 64), (128, 64)))
    inv_freq = 10000.0 ** (-np.arange(32, dtype=np.float64) / 32.0)
    ctx = np.arange(2048)
    pos = (ctx if s2 == 1 else np.maximum(ctx - 1024, 0)).astype(np.float64)
    ang = pos[:, None] * inv_freq[None, :]
    cos = np.cos(ang).astype(np.float32).reshape(16, 128, 32).transpose(1, 0, 2)
    sin = np.sin(ang).astype(np.float32).reshape(16, 128, 32).transpose(1, 0, 2)
    put("cos", cos.reshape(128, 512))
    put("sin", sin.reshape(128, 512))
    log_g = np.log1p(-np.exp2(-5.0 - np.arange(8, dtype=np.float64)))
    idx = np.arange(128)
    kj = idx[:, None]
    qi = idx[None, :]
    mask = ~((kj >= 64) & (qi < 64))
    DT = np.exp(np.abs(qi - kj)[None] * log_g[:, None, None]) * mask[None] * 0.125
    put("DT", DT.transpose(1, 0, 2).reshape(128, 1024))
    xi = np.exp((idx + 1.0)[None, :] * log_g[:, None])
    XiT = np.zeros((128, 4, 128))
    for j in range(4):
        XiT[0:64, j, :] = xi[2 * j][None, :]
        XiT[64:128, j, :] = xi[2 * j + 1][None, :]
    put("XiT", XiT.reshape(128, 512))
    zeta = np.exp((127.0 - idx)[:, None] * log_g[None, :]) * 0.125
    put("zeta", zeta)
    g128 = np.exp(128.0 * log_g)
    G = np.zeros((128, 4))
    for j in range(4):
        G[0:64, j] = g128[2 * j]
        G[64:128, j] = g128[2 * j + 1]
    put("g128", G)
    flag = np.ones((128, 16), np.float32)
    if s2 == 0:
        flag[:, 0:8] = 0.0
    put("flag", flag)
    put("ident", np.eye(128, dtype=np.float32))
    return cstv


def _perm_w_in(w):
    o1, o2, o3, o4, o5, o6, o7 = 512, 1024, 2048, 3072, 4096, 5120, 6144
    cols = []
    for j in range(4):
        hs = (2 * j, 2 * j + 1)
        for h in hs:
            cols.append(np.arange(h * 64, (h + 1) * 64))
        for h in hs:
            cols.append(o1 + np.arange(h * 64, (h + 1) * 64))
        for h in hs:
            cols.append(o2 + np.arange(h * 128, (h + 1) * 128))
        if j % 2 == 0:
            for h in range(2 * j, 2 * j + 4):
                cols.append(o3 + np.arange(h * 128, (h + 1) * 128))
    for h in range(8):
        cols.append(o4 + np.arange(h * 128, (h + 1) * 128))
        cols.append(o5 + np.arange(h * 128, (h + 1) * 128))
        cols.append(o6 + np.arange(h * 128, (h + 1) * 128))
        cols.append(o7 + np.arange(h * 128, (h + 1) * 128))
    cols = np.concatenate(cols)
    assert cols.shape[0] == 7168 and np.unique(cols).shape[0] == 7168
    return np.ascontiguousarray(w[:, cols])


_NC_CACHE = {}


def make_in_maps(x, p, attn_norm, w_in, ret_gn, diff_qn, diff_kn, diff_lq1, diff_lk1, diff_lq2, diff_lk2,
                 diff_subln, w_out, ple_norm, w_ple_gate, w_ple_proj, batches=range(4)):
    f = lambda a: np.asarray(a, np.float32)
    x, p = f(x), f(p)
    w_in_p = _perm_w_in(f(w_in)[0])
    w_out_ = np.ascontiguousarray(f(w_out)[0])
    w_gate_ = np.ascontiguousarray(f(w_ple_gate)[0])
    w_pp_ = np.ascontiguousarray(f(w_ple_proj)[0])
    pn_rep_ = np.ascontiguousarray(np.broadcast_to(f(ple_norm)[0].reshape(1, 2048), (128, 2048)))
    in_maps = []
    for b in batches:
        for s2 in range(2):
            if s2 == 1:
                xcv = np.ascontiguousarray(x[b])
            else:
                xcv = np.concatenate([np.zeros((1024, 2048), np.float32), x[b, 0:1024]], axis=0)
            cstv = _host_tables(s2, f(attn_norm)[0], f(ple_norm)[0], f(ret_gn)[0], f(diff_qn)[0], f(diff_kn)[0],
                                f(diff_lq1)[0], f(diff_lk1)[0], f(diff_lq2)[0], f(diff_lk2)[0], f(diff_subln)[0])
            in_maps.append({
                "xc": xcv,
                "xcT": np.ascontiguousarray(xcv.T),
                "pT": np.ascontiguousarray(p[0, b, s2 * 1024:(s2 + 1) * 1024, :].T),
                "w_in": w_in_p, "w_out": w_out_, "w_gate": w_gate_, "w_pp": w_pp_,
                "cst": cstv,
                "pn_rep": pn_rep_,
            })
    return in_maps


def kernel(**inputs):
    in_maps = make_in_maps(**inputs)
    if "nc" not in _NC_CACHE:
        _NC_CACHE["nc"] = build_nc()
    res = run_bass_kernel_spmd(_NC_CACHE["nc"], in_maps, core_ids=list(range(8)))
    out = np.zeros((4, 2048, 2048), np.float32)
    i = 0
    for b in range(4):
        for s2 in range(2):
            out[b, s2 * 1024:(s2 + 1) * 1024, :] = res.results[i]["y"]
            i += 1
    return out
```

```python
import contextlib
import numpy as np
import concourse.bass as bass
import concourse.mybir as mybir
from concourse.bass_utils import run_bass_kernel_spmd

F32 = mybir.dt.float32
JUNK = object()
BF16 = mybir.dt.bfloat16
AF = mybir.ActivationFunctionType
ALU = mybir.AluOpType
AX = mybir.AxisListType

EPS = 1e-6
LAM_INIT = 0.8 - 0.6 * 1.0

_CST = {}
_off = 0
for _n, _w in [("an_g", 16), ("pn_g", 16), ("retgn", 8), ("subln", 1), ("qn2", 1), ("kn2", 1),
               ("lq1", 64), ("lk1", 64), ("lq2", 64), ("lk2", 64), ("cos", 512), ("sin", 512),
               ("DT", 1024), ("XiT", 512), ("zeta", 8), ("g128", 4), ("flag", 16), ("ident", 128)]:
    _CST[_n] = (_off, _w)
    _off += _w
NCST = _off


class Sched:
    ENG = ["pe", "act", "dve", "pool", "sp"]

    def __init__(self, eng_sems, dma_sems):
        self.sem = dict(eng_sems)
        self.ndma = len(dma_sems)
        for i, s in enumerate(dma_sems):
            self.sem[("d", i)] = s
        self.stream = {e: [] for e in self.ENG}
        self.cnt = {k: 0 for k in self.sem}
        self.waited = {e: {} for e in self.ENG}
        self.lastw = {}
        self.readers = {}
        self.dma_next = {"sp": 0, "pool": 0}
        self.bar = []
        self.nbar = 0

    def barrier(self, trivial):
        keys = []
        for e, (fn, extra_w) in trivial.items():
            k = ("bar", self.nbar, e)
            self.op(e, fn, [], [k] + list(extra_w))
            keys.append(k)
        self.nbar += 1
        self.bar = keys

    def _deps(self, eng, reads, writes):
        deps = {}

        def need(k, c):
            if c > 0 and deps.get(k, 0) < c:
                deps[k] = c
        for r in reads:
            lw = self.lastw.get(r)
            if lw:
                need(*lw)
            if r in self.BANKS:
                for k, c in self.readers.get(r, {}).items():
                    if k != eng:
                        need(k, c)
        for w in writes:
            lw = self.lastw.get(w)
            if lw and (lw[0] != eng or eng != "pe"):
                need(*lw)
            for k, c in self.readers.get(w, {}).items():
                if k != eng or eng != "pe":
                    need(k, c)
        waits = []
        for k, c in deps.items():
            if self.waited[eng].get(k, 0) < c:
                self.waited[eng][k] = c
                waits.append((k, c))
        return waits

    @staticmethod
    def bankof(k):
        if not isinstance(k, tuple):
            return None
        n = k[0]
        if n == "zps":
            return "Z%d" % k[1]
        if n == "tps":
            return "T"
        if n in ("sps", "pps"):
            return ("A", "B", "Z1")[k[1]]
        if n == "pA":
            return "A"
        if n == "pB":
            return "B"
        if n == "kv":
            return "O0"
        if n == "Ob":
            return "O%d" % k[1]
        return None

    BANKS = ("Z0", "Z1", "T", "A", "B", "O0", "O1", "O2")

    def _norm(self, reads, writes):
        r2 = [self.bankof(k) or k for k in reads]
        w2 = [self.bankof(k) or k for k in writes]
        return r2, w2

    def op(self, eng, fn, reads=(), writes=(), inc=True):
        reads, writes = self._norm(reads, writes)
        reads = list(reads) + self.bar
        waits = self._deps(eng, reads, writes)
        c = self.cnt[eng] + 1
        if inc:
            self.cnt[eng] = c
        self.stream[eng].append((waits, fn, eng if inc else None))
        for r in reads:
            self.readers.setdefault(r, {})[eng] = c
        for w in writes:
            self.lastw[w] = (eng, c)
            self.readers[w] = {}

    def dma(self, q, fn, reads=(), writes=()):
        half = self.ndma // 2
        base = 0 if q == "sp" else half
        slot = base + self.dma_next[q]
        self.dma_next[q] = (self.dma_next[q] + 1) % half
        key = ("d", slot)
        reads, writes = self._norm(reads, writes)
        reads = list(reads) + self.bar
        waits = self._deps(q, reads, writes)
        prev = self.cnt[key]
        if prev > 0 and self.waited[q].get(key, 0) < prev:
            self.waited[q][key] = prev
            waits.append((key, prev))
        c = prev + 16
        self.cnt[key] = c
        self.stream[q].append((waits, fn, key))
        for r in reads:
            self.readers.setdefault(r, {})[key] = c
        for w in writes:
            self.lastw[w] = (key, c)
            self.readers[w] = {}

    def emit(self, name, eng):
        for waits, fn, inckey in self.stream[name]:
            for k, c in waits:
                eng.wait_ge(self.sem[k], c)
            ins = fn(eng)
            if inckey is not None:
                ins.then_inc(self.sem[inckey], 16 if isinstance(inckey, tuple) else 1)


def build_nc(stage=99, sub=99, nj=4):
    nc = bass.Bass("TRN2", target_bir_lowering=False)

    def din(n, s):
        return nc.dram_tensor(n, s, F32, kind="ExternalInput").ap()
    xc = din("xc", [2048, 2048])
    xcT = din("xcT", [2048, 2048])
    pT = din("pT", [256, 1024])
    w_in = din("w_in", [2048, 7168])
    w_out = din("w_out", [2048, 2048])
    w_gate = din("w_gate", [2048, 2048])
    w_pp = din("w_pp", [256, 2048])
    cst_d = din("cst", [128, NCST])
    pn_rep = din("pn_rep", [128, 2048])
    y = nc.dram_tensor("y", [1024, 2048], F32, kind="ExternalOutput").ap()

    cur = [(nc.sbuf_base + 63) // 64 * 64]
    top = nc.sbuf_top

    def nbytes(shape, dt):
        n = 1
        for s in shape[1:]:
            n *= s
        return n * (4 if dt == F32 else 2)

    def alloc(name, shape, dt, at=None):
        off = cur[0] if at is None else at
        t = nc.alloc_sbuf_tensor_at(name, shape, dt, offset=off)
        if at is None:
            cur[0] = (off + nbytes(shape, dt) + 63) // 64 * 64
            assert cur[0] <= top, (name, cur[0], top)
        return t

    R1 = cur[0]
    xgT = alloc("xgT", [128, 16, 2048], BF16)
    h1 = alloc("h1", [128, 8, 2048], F32, at=R1)
    wbuf = [alloc("wbuf0", [128, 16, 512], BF16), alloc("wbuf1", [128, 16, 512], BF16)]
    RA = cur[0]
    mixedT = alloc("mixedT", [128, 16, 1024], BF16)
    hnT = alloc("hnT", [128, 16, 1024], BF16, at=RA)
    xTs = [alloc("xT%d" % i, [128, 2048], F32, at=RA + 8192 * i) for i in range(4)]
    cst = alloc("cst", [128, NCST], F32)
    sm = alloc("sm", [128, 320], F32)
    identb = alloc("identb", [128, 128], BF16)
    mhalf = alloc("mhalf", [128, 16], F32)
    sge = alloc("sge", [128, 512], F32)
    junkf = alloc("junkf", [128, 128], F32)
    junks = [alloc("junk%d" % i, [128, 128], F32) for i in range(4)]
    mtok = [alloc("mtok%d" % i, [128, 128], BF16) for i in range(4)]
    PT = [alloc("PT%d" % i, [128, 512], BF16) for i in range(4)]
    RB = cur[0]
    rQT = alloc("rQT", [128, 1024], BF16)
    rQsT = alloc("rQsT", [128, 1024], BF16)
    rKT = alloc("rKT", [128, 2048], BF16)
    rKz = alloc("rKz", [128, 16, 128], BF16)
    rV = alloc("rV", [128, 16, 256], BF16)
    rsg = alloc("rsg", [128, 8, 512], BF16)
    zs = [alloc("zs0", [128, 256], F32), alloc("zs1", [128, 256], F32)]
    rE = alloc("rE", [128, 256], F32)
    rF = alloc("rF", [128, 256], F32)
    qkr = [alloc("qkr%d" % i, [128, 256], BF16) for i in range(3)]
    AT = [alloc("AT0", [128, 128], BF16), alloc("AT1", [128, 128], BF16)]
    Rst2 = [alloc("Rst_%d" % i, [128, 128], F32) for i in range(2)]
    Rb2 = [[alloc("Rb%d_%d" % (i, k), [128, 128], BF16) for k in range(2)] for i in range(2)]
    DQ0 = cur[0]
    dQT = [alloc("dQT0", [128, 1024], BF16), None]
    dKT = [alloc("dKT0", [128, 2048], BF16), None]
    Vaug = [alloc("Vaug0", [128, 16, 132], BF16), None]
    dsg = [alloc("dsg0", [128, 8, 128], BF16), None]
    zsq = [alloc("zsq0", [128, 256], F32), alloc("zsq1", [128, 256], F32)]
    qkn = [alloc("qkn%d" % i, [128, 256], BF16) for i in range(3)]
    d1 = [alloc("d1_0", [128, 128], F32), alloc("d1_1", [128, 128], F32)]
    assert cur[0] - DQ0 >= 16384
    xs = [alloc("xs0", [128, 2048], F32, at=DQ0), alloc("xs1", [128, 2048], F32, at=DQ0 + 8192)]
    _save = cur[0]
    cur[0] = RB
    dQT[1] = alloc("dQT1", [128, 1024], BF16)
    dKT[1] = alloc("dKT1", [128, 2048], BF16)
    Vaug[1] = alloc("Vaug1", [128, 16, 132], BF16)
    dsg[1] = alloc("dsg1", [128, 8, 128], BF16)
    Osb = alloc("Osb", [128, 3, 396], F32)
    dd4 = alloc("dd4", [128, 4, 128], F32)
    zsb = [alloc("zsb%d" % i, [128, 512], F32) for i in range(3)]
    assert cur[0] <= _save
    cur[0] = _save
    ysb = alloc("ysb", [128, 512], F32)
    RB_end = cur[0]
    cur[0] = RB
    xr = [alloc("xr%d" % i, [128, 512], F32) for i in range(2)]
    sig = [alloc("sig%d" % i, [128, 512], F32) for i in range(2)]
    ot = [alloc("ot%d" % i, [128, 512], F32) for i in range(2)]
    pTb = alloc("pTb", [128, 2, 1024], BF16)
    wpp = alloc("wpp", [128, 2, 2048], BF16)
    hnb = [alloc("hnb%d" % i, [128, 2048], BF16) for i in range(2)]
    gtab = alloc("gtab", [128, 2048], F32)
    assert cur[0] <= RB_end or cur[0] <= top
    cur[0] = max(cur[0], RB_end)

    def C(name, a=None, b=None):
        o, w = _CST[name]
        if a is None:
            return cst[:, o:o + w]
        return cst[:, o + a:o + b]

    SSQX, MSX, SQX, RSTD, EPSQ = 0, 16, 32, 48, 64
    EPSC, LAMS, QN2S, SUBS, KN2 = 80, 81, 90, 91, 92
    HSS, HSQ, HR = 96, 104, 112
    FBASE = 128
    NRSTD = 260

    def smc(i, n=1):
        return sm[:, i:i + n]

    with contextlib.ExitStack() as es:
        eng_sems = {e: es.enter_context(nc.semaphore("s_" + e)) for e in ["pe", "act", "dve", "pool"]}
        dma_sems = [es.enter_context(nc.semaphore("s_d%d" % i)) for i in range(24)]
        S = Sched(eng_sems, dma_sems)
        zps = [es.enter_context(nc.psum_tensor("zps%d" % i, [128, 512], F32)) for i in range(2)]
        tps = es.enter_context(nc.psum_tensor("tps", [128, 1024], BF16))
        pA = es.enter_context(nc.psum_tensor("pA", [128, 512], F32))
        pB = es.enter_context(nc.psum_tensor("pB", [128, 512], F32))
        tpsB = pB.bitcast(BF16)
        pO = [es.enter_context(nc.psum_tensor("pO%d" % i, [128, 512], F32)) for i in range(3)]

        jk_i = [0]

        def ACT(out, in_, func, reads, writes, scale=None, bias=None, accum=None):
            if out is JUNK:
                jk_i[0] += 1
                out = junks[jk_i[0] % 4][:]
                writes = [w for w in writes if w != "junkf"] + [("junk", jk_i[0] % 4)]
            kw = {}
            if scale is not None:
                kw["scale"] = scale
            if bias is not None:
                kw["bias"] = bias
            if accum is not None:
                kw["accum_out"] = accum
            S.op("act", lambda e: e.activation(out=out, in_=in_, func=func, **kw), reads, writes)

        def TS(out, in0, s1, s2, op0, op1, reads, writes, eng="dve"):
            if op1 is None:
                S.op(eng, lambda e: e.tensor_scalar(out=out, in0=in0, scalar1=s1, scalar2=None, op0=op0), reads, writes)
            else:
                S.op(eng, lambda e: e.tensor_scalar(out=out, in0=in0, scalar1=s1, scalar2=s2, op0=op0, op1=op1), reads, writes)

        def TT(out, in0, in1, op, reads, writes, eng="dve"):
            S.op(eng, lambda e: e.tensor_tensor(out=out, in0=in0, in1=in1, op=op), reads, writes)

        def STT(out, in0, scalar, in1, op0, op1, reads, writes):
            S.op("dve", lambda e: e.scalar_tensor_tensor(out=out, in0=in0, scalar=scalar, in1=in1, op0=op0, op1=op1), reads, writes)

        def RED(out, in_, reads, writes):
            S.op("dve", lambda e: e.tensor_reduce(out=out, in_=in_, axis=AX.X, op=ALU.add), reads, writes)

        def RECIP(out, in_, reads, writes):
            S.op("dve", lambda e: e.reciprocal(out=out, in_=in_), reads, writes)

        def RSQ(out, in_, n, reads, writes):
            S.op("pool", lambda e: e.tensor_tensor(out=out, in0=in_, in1=mhalf[:, 0:n], op=ALU.pow),
                 list(reads) + ["mhalf"], writes)

        def SILU_SB(out_bf, y_ap, ykey, width, writes):
            e_ap = sge[:, 0:width]
            ACT(e_ap, y_ap, AF.Exp, [ykey], ["sge"], scale=-1.0)
            TS(e_ap, e_ap, 1.0, None, ALU.add, None, ["sge"], ["sge"])
            RECIP(e_ap, e_ap, ["sge"], ["sge"])
            TT(out_bf, y_ap, e_ap, ALU.mult, [ykey, "sge"], writes)

        def MM(out, lhsT, rhs, start, stop, reads, writes, inc, sgc=False):
            if sgc:
                S.op("pe", lambda e: e.matmul(out, lhsT, rhs, start=start, stop=stop, skip_group_check=True),
                     reads, writes, inc=inc)
            else:
                S.op("pe", lambda e: e.matmul(out, lhsT, rhs, start=start, stop=stop), reads, writes, inc=inc)

        def TR(out, in_, reads, writes):
            S.op("pe", lambda e: e.transpose(out, in_, identb[:]), list(reads) + ["identb"], writes)

        def DMA(q, out, in_, reads, writes):
            S.dma(q, lambda e: e.dma_start(out=out, in_=in_), reads, writes)

        def MEMSET(eng, ap, val, writes):
            S.op(eng, lambda e: e.memset(ap, val), [], writes)

        tps_i = [0]

        def tslot():
            i = tps_i[0] % 8
            tps_i[0] += 1
            return i, tps[:, i * 128:(i + 1) * 128], ("tps", i)

        f_i = [0]

        def fcols(n):
            i = f_i[0] % 32
            f_i[0] += 1
            return sm[:, FBASE + i * 4:FBASE + i * 4 + n], ("fs", i)

        m_i = [0]

        groups = []
        col = 0
        GA, G3, GD = [], [], []
        for j in range(4):
            GA.append(len(groups)); groups.append((w_in[:, col:col + 512], 512)); col += 512
            if j % 2 == 0:
                G3.append(len(groups)); groups.append((w_in[:, col:col + 512], 512)); col += 512
        for h in range(8):
            GD.append(len(groups)); groups.append((w_in[:, col:col + 512], 512)); col += 512
        assert col == 7168
        GO = []
        for n in range(4):
            GO.append(len(groups)); groups.append((w_out[:, n * 512:(n + 1) * 512], 512))
        GG = []
        for n in range(4):
            GG.append(len(groups)); groups.append((w_gate[:, n * 512:(n + 1) * 512], 512))
        loaded = [0]

        def load_group(g):
            ap, ncols = groups[g]
            slot = g % 2
            apr = ap.rearrange("(kc p) n -> p kc n", p=128)
            for part in range(4):
                DMA("pool", wbuf[slot][:, part * 4:part * 4 + 4, 0:ncols], apr[:, part * 4:part * 4 + 4, :], [],
                    [("wbuf", slot, part)])

        def ensure_loaded(g):
            while loaded[0] <= min(g + 1, len(groups) - 1):
                load_group(loaded[0])
                loaded[0] += 1

        DMA("sp", cst[:], cst_d, [], ["cst"])
        ensure_loaded(0)
        MEMSET("dve", smc(EPSC), EPS, [("sm", "eps")])
        TT(identb[:], C("ident"), C("ident"), ALU.max, ["cst"], ["identb"])
        TT(junkf[:, 0:64], C("lq1"), C("lk1"), ALU.mult, ["cst"], ["junkf"])
        RED(smc(LAMS + 0), junkf[:, 0:64], ["junkf"], [("sm", "l0")])
        TT(junkf[:, 64:128], C("lq2"), C("lk2"), ALU.mult, ["cst"], ["junkf2"])
        RED(smc(LAMS + 1), junkf[:, 64:128], ["junkf2"], [("sm", "l1")])
        ACT(smc(LAMS + 2, 2), smc(LAMS + 0, 2), AF.Exp, [("sm", "l0"), ("sm", "l1")], [("sm", "l2")])
        TT(smc(LAMS + 4), smc(LAMS + 2), smc(LAMS + 3), ALU.subtract, [("sm", "l2")], [("sm", "l4")])
        TS(smc(LAMS + 5), smc(LAMS + 4), -1.0, -LAM_INIT, ALU.mult, ALU.add, [("sm", "l4")], [("sm", "neglam")])
        NEGLAM = smc(LAMS + 5)
        TS(smc(QN2S), C("qn2"), C("kn2"), 0.125, ALU.mult, ALU.mult, ["cst"], [("sm", "qn2s")])
        TS(smc(SUBS), C("subln"), 1.0 - LAM_INIT, None, ALU.mult, None, ["cst"], [("sm", "subs")])

        MEMSET("pool", mhalf[:], -0.5, ["mhalf"])
        for kc in range(16):
            DMA("sp", xTs[kc % 4][:], xcT[kc * 128:(kc + 1) * 128, :], [], [("xT", kc % 4)])
            if kc % 2 == 0:
                TS(xgT[:, kc, :], xTs[kc % 4][:], C("an_g", kc, kc + 1), None, ALU.mult, None,
                   [("xT", kc % 4), "cst"], [("R1k", kc)])
            else:
                ACT(xgT[:, kc, :], xTs[kc % 4][:], AF.Copy, [("xT", kc % 4), "cst"], [("R1k", kc)],
                    scale=C("an_g", kc, kc + 1))
        for t in range(2):
            DMA("sp", xs[t % 2][:], xc[t * 128:(t + 1) * 128, :], [], [("xs", t % 2)])

        def stats(t):
            ACT(xs[t % 2][:], xs[t % 2][:], AF.Square, [("xs", t % 2)], [("xs", t % 2), ("ssqx", t)], accum=smc(SSQX + t))
            TS(smc(MSX + t), smc(SSQX + t), 1.0 / 2048, EPS, ALU.mult, ALU.add, [("ssqx", t)], [("msx", t)])
            RSQ(smc(RSTD + t), smc(MSX + t), 1, [("msx", t)], [("rstd", t)])
            if t + 2 < 16:
                DMA("sp", xs[t % 2][:], xc[(t + 2) * 128:(t + 3) * 128, :], [], [("xs", t % 2)])

        def rstd(t):
            return smc(RSTD + t)

        zi = [0]

        def proj_part(t, g, c0, c1, i, k0, k1):
            wb = wbuf[g % 2]
            for kc in range(k0, k1):
                MM(zps[i][:, c0:c1], xgT[:, kc, t * 128:(t + 1) * 128], wb[:, kc, c0:c1], kc == 0, kc == 15,
                   [("wbuf", g % 2, kc // 4), "R1", ("R1k", kc)], [("zps", i)], kc == 15)

        def proj(t, g, c0, c1, mid=None):
            i = zi[0] % 2
            zi[0] += 1
            proj_part(t, g, c0, c1, i, 0, 8)
            if mid is not None:
                mid()
            proj_part(t, g, c0, c1, i, 8, 16)
            return zps[i], ("zps", i)

        def finalize(o_ap, o_key, sg_ap, sg_key, gain_ap, gain_key, fchunk, ot_idx):
            fc, fk = fcols(3)
            ACT(junkf[:], o_ap, AF.Square, [o_key], ["junkf", fk], accum=fc[:, 0:1])
            ACT(fc[:, 1:2], fc[:, 0:1], AF.Sqrt, [fk, ("sm", "eps")], [fk], scale=1.0 / 128, bias=smc(EPSC))
            RECIP(fc[:, 2:3], fc[:, 1:2], [fk], [fk])
            mi = m_i[0] % 2
            m_i[0] += 1
            STT(mtok[mi][:], o_ap, fc[:, 2:3], sg_ap, ALU.mult, ALU.mult, [o_key, fk, sg_key], [("mtok", mi)])
            ti, tap, tk = tslot()
            TR(tap, mtok[mi][:], [("mtok", mi)], [tk])
            ACT(mixedT[:, fchunk, ot_idx * 128:(ot_idx + 1) * 128], tap, AF.Copy, [tk, gain_key],
                [("mixT", fchunk)], scale=gain_ap)

        marks = []

        def phase_barrier():
            marks.append(len(S.stream["pe"]))
            S.barrier({
                "pe": (lambda e: e.transpose(tps[:, 0:128], identb[:], identb[:]), [("tps", 0)]),
                "act": (lambda e: e.activation(out=sm[:, 300:301], in_=sm[:, 301:302], func=AF.Copy), []),
                "dve": (lambda e: e.memset(sm[:, 302:303], 0.0), []),
                "pool": (lambda e: e.memset(sm[:, 303:304], 0.0), []),
            })

        MEMSET("dve", sm[:, 300:304], 0.0, ["smbar"])
        o_c, o_s, o_z, o_x, o_d = (_CST[n][0] for n in ("cos", "sin", "zeta", "XiT", "DT"))

        def COPY(out, in_, reads, writes):
            S.op("dve", lambda e: e.tensor_copy(out=out, in_=in_), reads, writes)

        def ret_pair(j, carry):
            Rst = Rst2[j % 2]
            Rb = Rb2[j % 2]
            RK = "Rst%d" % (j % 2)

            def rbk(k):
                return ("Rb", j % 2, k)
            g = GA[j]
            ensure_loaded(g)
            g3 = G3[j // 2] if j % 2 == 0 else None
            MEMSET("dve", Rst[:], 0.0, [RK])
            MEMSET("dve", Rb[0][:], 0.0, [rbk(0)])

            st_r = {}

            def rA(t):
                own = t >= 8
                c0 = 0 if own else 128
                G = 4 if own else 2
                if j == 0:
                    stats(t)
                ps, pk = proj(t, g, c0, 512, mid=lambda: small_ops(t))
                if own and g3 is not None:
                    ps3, pk3 = proj(t, g3, 0, 512)
                z = zs[t % 2]
                ACT(z[:, c0:256], ps[:, c0:256], AF.Copy, [pk, ("rstd", t)], [("zs", t % 2)], scale=rstd(t))
                ACT(rV[:, t, :], ps[:, 256:512], AF.Copy, [pk, ("rstd", t)], [("rV", t)], scale=rstd(t))
                if own and g3 is not None:
                    ACT(rsg[:, t - 8, :], ps3[:, :], AF.Silu, [pk3, ("rstd", t)], [("rsg", t - 8)], scale=rstd(t))
                zv = z[:, c0:256].rearrange("p (g h d) -> p g h d", g=G, h=2)
                Ev = rE[:, c0:256].rearrange("p (g h d) -> p g h d", g=G, h=2)
                Fv = rF[:, c0:256].rearrange("p (g h d) -> p g h d", g=G, h=2)
                qv = qkr[(t + j) % 3][:, c0:256].rearrange("p (g h d) -> p g h d", g=G, h=2)
                cosb = cst[:, o_c + t * 32:o_c + (t + 1) * 32].unsqueeze(1).unsqueeze(1).broadcast_to([128, G, 2, 32])
                sinb = cst[:, o_s + t * 32:o_s + (t + 1) * 32].unsqueeze(1).unsqueeze(1).broadcast_to([128, G, 2, 32])
                TT(Ev, zv, cosb, ALU.mult, [("zs", t % 2), "cst"], ["rE"], eng="pool")
                TT(Fv, zv, sinb, ALU.mult, [("zs", t % 2), "cst"], ["rF"], eng="pool")
                TT(qv[:, :, 0, :], Ev[:, :, 0, :], Fv[:, :, 1, :], ALU.subtract, ["rE", "rF"], [("qkr", (t + j) % 3)])
                TT(qv[:, :, 1, :], Fv[:, :, 0, :], Ev[:, :, 1, :], ALU.add, ["rE", "rF"], [("qkr", (t + j) % 3)])
                if t < 15:
                    zb = cst[:, o_z + 2 * j:o_z + 2 * j + 2].unsqueeze(2).broadcast_to([128, 2, 64])
                    TT(rKz[:, t, :].rearrange("p (a d) -> p a d", a=2),
                       qkr[(t + j) % 3][:, 128:256].rearrange("p (a d) -> p a d", a=2), zb, ALU.mult,
                       [("qkr", (t + j) % 3), "cst"], [("rKz", t)], eng="pool")

            def rC(t):
                own = t >= 8
                if not own:
                    return
                TR(tps[:, 0:128], qkr[(t + j) % 3][:, 128:256], [("qkr", (t + j) % 3)], [("tps", 0)])
                TR(tps[:, 128:256], qkr[(t + j) % 3][:, 0:128], [("qkr", (t + j) % 3)], [("tps", 0)])
                ACT(rKT[:, (t - 8) * 128:(t - 7) * 128], tps[:, 0:128], AF.Copy, [("tps", 0)], [("rKT", t)])
                ACT(rQT[:, (t - 8) * 128:(t - 7) * 128], tps[:, 128:256], AF.Copy, [("tps", 0)], [("rQT", t)])
                TT(rQsT[:, (t - 8) * 128:(t - 7) * 128], rQT[:, (t - 8) * 128:(t - 7) * 128],
                   cst[:, o_x + j * 128:o_x + (j + 1) * 128], ALU.mult, [("rQT", t), "cst"], [("rQsT", t)], eng="pool")

            def rD1(t):
                own = t >= 8
                for hl in range(2):
                    P0 = hl * 64
                    if own:
                        MM((pA, pB)[hl][:, 0:128], rKT[P0:P0 + 64, (t - 8) * 128:(t - 7) * 128],
                           rQT[P0:P0 + 64, (t - 8) * 128:(t - 7) * 128], True, True,
                           [("rKT", t), ("rQT", t)], [(("pA", 0), ("pB", 0))[hl]], True)
                    if t < 15:
                        MM(pO[0][P0:P0 + 64, 0:128], rKz[:, t, hl * 64:(hl + 1) * 64], rV[:, t, hl * 128:(hl + 1) * 128],
                           True, True, [("rKz", t), ("rV", t)], [("kv", hl)], True)
                if own:
                    for hl in range(2):
                        h = 2 * j + hl
                        TT(AT[hl][:], (pA, pB)[hl][:, 0:128], cst[:, o_d + h * 128:o_d + (h + 1) * 128], ALU.mult,
                           [(("pA", 0), ("pB", 0))[hl], "cst"], [("AT", hl)])
                if t < 15:
                    STT(Rb[(t + 1) % 2][:], Rst[:], C("g128", j, j + 1), pO[0][:, 0:128], ALU.mult, ALU.add,
                        [RK, "cst", ("kv", 0), ("kv", 1)], [rbk((t + 1) % 2)])
                    STT(Rst[:], Rst[:], C("g128", j, j + 1), pO[0][:, 0:128], ALU.mult, ALU.add,
                        [RK, "cst", ("kv", 0), ("kv", 1)], [RK])

            def rD2(t):
                fa, fka = fcols(4)
                fb, fkb = fcols(4)
                for hl in range(2):
                    P0 = hl * 64
                    o_ap = pO[1 + hl][:, 0:128]
                    okey = ("Ob", 1 + hl)
                    MM(o_ap, AT[hl][:], rV[:, t, hl * 128:(hl + 1) * 128], True, False,
                       [("AT", hl), ("rV", t)], [okey], False)
                    MM(o_ap, rQsT[P0:P0 + 64, (t - 8) * 128:(t - 7) * 128], Rb[t % 2][P0:P0 + 64, :], False, True,
                       [("rQsT", t), rbk(t % 2)], [okey], True)
                for hl in range(2):
                    ACT(JUNK, pO[1 + hl][:, 0:128], AF.Square, [("Ob", 1 + hl)], ["junkf", fka], accum=fa[:, hl:hl + 1])
                TS(fb[:, 0:2], fa[:, 0:2], 1.0 / 128, EPS, ALU.mult, ALU.add, [fka], [fkb])
                RSQ(fa[:, 2:4], fb[:, 0:2], 2, [fkb], [fka])
                st_r[t] = (fa, fka)

            def rD2b(t):
                fa, fka = st_r[t]
                for hl in range(2):
                    h = 2 * j + hl
                    STT(mtok[hl][:], pO[1 + hl][:, 0:128], fa[:, 2 + hl:3 + hl],
                        rsg[:, t - 8, (h % 4) * 128:(h % 4 + 1) * 128], ALU.mult, ALU.mult,
                        [("Ob", 1 + hl), fka, ("rsg", t - 8)], [("mtok", hl)])

            def rEe(t):
                for hl in range(2):
                    TR(tps[:, hl * 128:(hl + 1) * 128], mtok[hl][:], [("mtok", hl)], [("tps", 0)])
                for hl in range(2):
                    h = 2 * j + hl
                    ACT(mixedT[:, h, (t - 8) * 128:(t - 7) * 128], tps[:, hl * 128:(hl + 1) * 128], AF.Copy,
                        [("tps", 0), "cst"], [("mixT", h, t - 8)], scale=C("retgn", h, h + 1))

            def small_ops(s_):
                if s_ < len(carry):
                    carry[s_]()
                if 8 <= s_ - 5 < 16:
                    rEe(s_ - 5)
                if 8 <= s_ - 4 < 16:
                    rD2(s_ - 4)
                    rD2b(s_ - 4)
                if 0 <= s_ - 3 < 16:
                    rD1(s_ - 3)
                if 0 <= s_ - 2 < 16:
                    rC(s_ - 2)

            for s_ in range(16):
                rA(s_)
            return [(lambda s_=s_: small_ops(s_)) for s_ in range(16, 16 + 5)]


        carry = []
        for j in (range(nj) if stage >= 1 else []):
            carry = ret_pair(j, carry)
        for f in carry:
            f()

        phase_barrier()
        OREG = [(0, 0), (0, 132), (0, 264), (1, 0), (1, 132), (1, 264), (2, 0), (2, 132)]
        if stage >= 2:
            TT(Vaug[0][:, :, 128:129], C("flag").unsqueeze(2), C("flag").unsqueeze(2), ALU.max, ["cst"], [("vflag", 0)])
            TT(Vaug[1][:, :, 128:129], C("flag").unsqueeze(2), C("flag").unsqueeze(2), ALU.max, ["cst"], [("vflag", 1)])

        def d_proj_thunks(h):
            g = GD[h]
            bs = h % 2
            st = {}

            def tile_thunks(t):
                own = t >= 8
                c0 = 0 if own else 128
                G = 4 if own else 2
                zb_ = zsb[t % 3]
                zk = ("zsb", t % 3)
                zq = zsq[t % 2]

                def a0():
                    if t == 0:
                        ensure_loaded(g)
                    i = 0
                    st[("zi", t)] = i
                    proj_part(t, g, c0, 512 if own else 384, i, 0, 8)

                def a0b():
                    i = st[("zi", t)]
                    proj_part(t, g, c0, 512 if own else 384, i, 8, 16)
                    st[t] = (zps[i], ("zps", i))

                def a1():
                    ps, pk = st[t]
                    if own:
                        ACT(zb_[:, 0:512], ps[:, 0:512], AF.Copy, [pk, ("rstd", t)], [zk], scale=rstd(t))
                    else:
                        ACT(zb_[:, 128:384], ps[:, 128:384], AF.Copy, [pk, ("rstd", t)], [zk], scale=rstd(t))
                    TS(Vaug[bs][:, t, 0:128], zb_[:, 256:384], 1.0, 1.0, ALU.mult, ALU.mult, [zk], [("Vaug", bs, t)],
                       eng="pool")

                def a2():
                    TT(zq[:, c0:256], zb_[:, c0:256], zb_[:, c0:256], ALU.mult, [zk], [("zsq", t % 2)])
                    fc, fk = fcols(4)
                    fc2, fk2 = fcols(4)
                    st[("fc", t)] = (fc, fk)
                    RED(fc[:, 0:G], zq[:, c0:256].rearrange("p (g d) -> p g d", g=G), [("zsq", t % 2)], [fk])
                    TS(fc2[:, 0:G], fc[:, 0:G], 1.0 / 64, EPS, ALU.mult, ALU.add, [fk], [fk2])
                    RSQ(fc[:, 0:G], fc2[:, 0:G], G, [fk2], [fk])

                def a3():
                    fc, fk = st[("fc", t)]
                    if own:
                        ACT(sge[:, 0:128], zb_[:, 384:512], AF.Exp, [zk], ["sge"], scale=-1.0)
                    TT(qkn[t % 3][:, c0:256].rearrange("p (g d) -> p g d", g=G),
                       zb_[:, c0:256].rearrange("p (g d) -> p g d", g=G),
                       fc[:, 0:G].unsqueeze(2).broadcast_to([128, G, 64]), ALU.mult, [zk, fk], [("qkn", t % 3)])
                    if own:
                        TS(sge[:, 0:128], sge[:, 0:128], 1.0, None, ALU.add, None, ["sge"], ["sge"])

                def a4():
                    if own:
                        RECIP(sge[:, 0:128], sge[:, 0:128], ["sge"], ["sge"])
                        TT(dsg[bs][:, t - 8, :], zb_[:, 384:512], sge[:, 0:128], ALU.mult, [zk, "sge"], [("dsg", bs, t - 8)])
                return [a0, a0b, a1, a2, a3, a4] if own else [a0, a0b, a1, a2, a3]

            def dC(t):
                own = t >= 8
                TR(tps[:, 0:128], qkn[t % 3][:, 128:256], [("qkn", t % 3)], [("tps", 0)])
                if own:
                    TR(tps[:, 128:256], qkn[t % 3][:, 0:128], [("qkn", t % 3)], [("tps", 0)])
                ACT(dKT[bs][:, t * 128:(t + 1) * 128], tps[:, 0:128], AF.Copy, [("tps", 0)], [("dKT", bs, t)])
                if own:
                    ACT(dQT[bs][:, (t - 8) * 128:(t - 7) * 128], tps[:, 128:256], AF.Copy, [("tps", 0), ("sm", "qn2s")],
                        [("dQT", bs, t)], scale=smc(QN2S))

            th = []
            for s_ in range(18):
                if s_ < 16:
                    tt_ = tile_thunks(s_)
                    th.append(tt_[0])
                if 0 <= s_ - 2 < 16:
                    th.append(lambda t=s_ - 2: dC(t))
                if s_ < 16:
                    th.extend(tt_[1:])
            return th

        def d_attn_thunks(h):
            bs = h % 2
            steps = [(qt, c, kb) for qt in range(2) for c in range(2) for kb in range(8 + 4 * qt + 4)]

            def oreg(c, qb):
                b_, o_ = OREG[c * 4 + qb]
                return pO[b_][:, o_:o_ + 129], ("Ob", b_)

            def emitS(i):
                qt, c, kb = steps[i]
                t0 = 8 + 4 * qt
                qb0 = max(kb - t0, 0)
                qlo = qb0 * 128
                P0 = c * 64
                si = i % 3
                pi = i % 4
                sps = (pA, pB, zps[1])[si]
                MM(sps[:, qlo:512], dKT[bs][P0:P0 + 64, kb * 128:(kb + 1) * 128],
                   dQT[bs][P0:P0 + 64, qt * 512 + qlo:(qt + 1) * 512], True, True,
                   [("dKT", bs, kb)] + [("dQT", bs, t0 + q) for q in range(qb0, 4)], [("sps", si)], True)
                ACT(PT[pi][:, qlo:512], sps[:, qlo:512], AF.Exp, [("sps", si)], [("PT", pi)])
                if kb >= t0:
                    ACT(PT[pi][64:128, qlo:qlo + 64], PT[pi][64:128, qlo:qlo + 64], AF.Copy, [("PT", pi)], [("PT", pi)],
                        scale=0.0)

            def emitAV(i):
                qt, c, kb = steps[i]
                t0 = 8 + 4 * qt
                qb0 = max(kb - t0, 0)
                pi = i % 4
                for qb in range(qb0, 4):
                    oa, ok = oreg(c, qb)
                    last = (kb == t0 + qb)
                    first_in_bank = (c * 4 + qb) in (0, 3, 6)
                    MM(oa, PT[pi][:, qb * 128:(qb + 1) * 128], Vaug[bs][:, kb, 0:129], kb == 0 and first_in_bank, last,
                       [("PT", pi), ("Vaug", bs, kb), ("vflag", bs)], [ok], last or qb == 3, sgc=True)

            def fin_thunks(qt):
                t0 = 8 + 4 * qt
                fa, fka = sm[:, 276 + 8 * qt:280 + 8 * qt], ("finA", qt)
                fb, fkb = sm[:, 280 + 8 * qt:284 + 8 * qt], ("finB", qt)
                out = []

                def f0():
                    for b_ in range(3):
                        nr = 3 if b_ < 2 else 2
                        COPY(Osb[:, b_, 0:nr * 132].rearrange("p (r c) -> p r c", c=132)[:, :, 0:129],
                             pO[b_][:, 0:nr * 132].rearrange("p (r c) -> p r c", c=132)[:, :, 0:129],
                             [("Ob", b_)], [("Osb", b_)])
                    RECIP(sm[:, 310:318], Osb[:].rearrange("p b c -> p (b c)")[:, 128:128 + 132 * 8:132],
                          [("Osb", 0), ("Osb", 1), ("Osb", 2)], ["rl8"])
                    TS(sm[:, 314:318], sm[:, 314:318], NEGLAM, None, ALU.mult, None, ["rl8", ("sm", "neglam")], ["rl8"])
                out.append(f0)
                for qb in range(4):
                    def f1(qb=qb):
                        b0, of0 = OREG[qb]
                        b1, of1 = OREG[4 + qb]
                        di = qb % 2
                        TS(d1[di][:], Osb[:, b0, of0:of0 + 128], sm[:, 310 + qb:311 + qb], None, ALU.mult, None,
                           [("Osb", b0), "rl8"], [("d1", di)])
                        STT(dd4[:, qb, :], Osb[:, b1, of1:of1 + 128], sm[:, 314 + qb:315 + qb], d1[di][:], ALU.mult, ALU.add,
                            [("Osb", b1), "rl8", ("d1", di)], [("dd4", qb)])
                        jk_i[0] += 1
                        jq = jk_i[0] % 4
                        TT(junks[jq][:], dd4[:, qb, :], dd4[:, qb, :], ALU.mult, [("dd4", qb)], [("junk", jq)])
                        RED(fa[:, qb:qb + 1], junks[jq][:], [("junk", jq)], [fka])
                    out.append(f1)

                def f2():
                    TS(fb[:, 0:4], fa[:, 0:4], 1.0 / 128, EPS, ALU.mult, ALU.add, [fka], [fkb])
                    RSQ(fa[:, 0:4], fb[:, 0:4], 4, [fkb], [fka])
                out.append(f2)
                for qb in range(4):
                    def f3a(qb=qb):
                        tq = t0 + qb
                        STT(mtok[qb][:], dd4[:, qb, :], fa[:, qb:qb + 1], dsg[bs][:, tq - 8, :], ALU.mult, ALU.mult,
                            [("dd4", qb), fka, ("dsg", bs, tq - 8)], [("mtok", qb)])
                    out.append(f3a)
                for qb in range(4):
                    def f3b(qb=qb):
                        tq = t0 + qb
                        mi = qb % 2
                        TR(tps[:, mi * 128:(mi + 1) * 128], mtok[qb][:], [("mtok", qb)], [("tps", 0)])
                        ACT(mixedT[:, 8 + h, (tq - 8) * 128:(tq - 7) * 128], tps[:, mi * 128:(mi + 1) * 128], AF.Copy,
                            [("tps", 0), ("sm", "subs")], [("mixT", 8 + h, tq - 8)], scale=smc(SUBS))
                    out.append(f3b)
                return out

            th = [lambda: (emitS(0), emitS(1))]
            pos_end = {}
            for i in range(len(steps)):
                def f(i=i):
                    if i + 2 < len(steps):
                        emitS(i + 2)
                    emitAV(i)
                th.append(f)
                qt, c, kb = steps[i]
                if c == 1 and kb == 8 + 4 * qt + 3:
                    pos_end[qt] = len(th)
            fin0 = fin_thunks(0)
            fin1 = fin_thunks(1)
            p0 = pos_end[0]
            out = (th[:p0] + fin0[:6] + th[p0:p0 + 6] + fin0[6:10] + th[p0 + 6:p0 + 12] + fin0[10:]
                   + th[p0 + 12:] + fin1[:6])
            return out, fin1[6:]

        def merge(A, P, frac=0.82):
            if not P:
                return list(A)
            if not A:
                return list(P)
            out = []
            ip = 0
            span = max(1.0, frac * len(A))
            for ia, a in enumerate(A):
                out.append(a)
                while ip < len(P) and (ip + 1) * span <= (ia + 1) * len(P):
                    out.append(P[ip])
                    ip += 1
            out.extend(P[ip:])
            return out

        if stage >= 2:
            for f in d_proj_thunks(0):
                f()
            tail = []
            for h in range(8):
                A_, nt = d_attn_thunks(h)
                A_ = A_[:6] + tail[:4] + A_[6:12] + tail[4:] + A_[12:]
                tail = nt
                P_ = d_proj_thunks(h + 1) if h + 1 < 8 else []
                for f in merge(A_, P_):
                    f()
            for f in tail:
                f()

        phase_barrier()
        x_i = [0]
        if stage >= 4:
            DMA("pool", pTb[:], pT.rearrange("(kc p) n -> p kc n", p=128), [], ["pTb"])
            DMA("pool", wpp[:], w_pp.rearrange("(kc p) n -> p kc n", p=128), [], ["wpp"])
            DMA("sp", gtab[:], pn_rep, [], ["gtab"])

        def h_stats(t):
            ACT(hnb[t % 2][:], h1[:, t, :], AF.Square, [("h1", t), "R1"], [("hnb", t % 2), ("hss", t)], accum=smc(HSS + t))
            TS(smc(HSQ + t), smc(HSS + t), 1.0 / 2048, EPS, ALU.mult, ALU.add, [("hss", t)], [("hsq", t)])
            RSQ(smc(HR + t), smc(HSQ + t), 1, [("hsq", t)], [("hr", t)])

        def h_norm(t):
            STT(hnb[t % 2][:], h1[:, t, :], smc(HR + t), gtab[:], ALU.mult, ALU.mult,
                [("h1", t), "R1", ("hr", t), "gtab"], [("hnb", t % 2)])

        def h_tr(t):
            for half in range(2):
                tb, tk = ((tps, ("tps", 0)), (tpsB, ("pB", 0)))[half]
                for k in range(8):
                    TR(tb[:, k * 128:(k + 1) * 128], hnb[t % 2][:, (half * 8 + k) * 128:(half * 8 + k + 1) * 128],
                       [("hnb", t % 2)], [tk])
                ACT(hnT[:, half * 8:half * 8 + 8, t * 128:(t + 1) * 128],
                    tb[:, :].rearrange("p (k n) -> p k n", k=8), AF.Copy, [tk],
                    [("hnT", t)] + [("mixT", f, t) for f in range(16)])

        for n in (range(4) if stage >= 3 else []):
            g = GO[n]
            ensure_loaded(g)
            for t in range(8):
                xi = x_i[0] % 2
                x_i[0] += 1
                DMA("sp", xr[xi][:], xc[(8 + t) * 128:(9 + t) * 128, n * 512:(n + 1) * 512], [], [("xr", xi)])
                i = zi[0] % 2
                zi[0] += 1
                for fc_ in range(16):
                    MM(zps[i][:, :], mixedT[:, fc_, t * 128:(t + 1) * 128], wbuf[g % 2][:, fc_, :], fc_ == 0, fc_ == 15,
                       [("wbuf", g % 2, fc_ // 4), ("mixT", fc_, t)], [("zps", i)], fc_ == 15)
                    if fc_ == 7 and n == 3 and stage >= 4 and t >= 2:
                        h_tr(t - 2)
                TT(h1[:, t, n * 512:(n + 1) * 512], zps[i][:, :], xr[xi][:], ALU.add, [("zps", i), ("xr", xi)],
                   ["R1"] if n == 0 and t == 0 else [("h1", t)])
                if n == 3 and stage >= 4:
                    if t >= 1:
                        h_norm(t - 1)
                    h_stats(t)
        if stage >= 4:
            h_tr(6)
            h_norm(7)
            h_tr(7)
        yv = y.rearrange("(a p) n -> p a n", p=128)
        if stage == 0:
            DMA("sp", yv[:, 0, 0:320], sm[:], [("sm", "neglam")], [])
            DMA("pool", yv[:, 1, :], xgT[:, 0, :], ["R1"], [])
            DMA("pool", yv[:, 2, :], xgT[:, 15, :], ["R1"], [])
        if stage in (1, 2):
            phase_barrier()
            for a in range(8):
                DMA("pool", yv[:, a, :].rearrange("p (c n) -> p c n", c=2), mixedT[:, 2 * a:2 * a + 2, :], [], [])
        if stage == 3:
            phase_barrier()
            DMA("sp", yv, h1[:], [], [])
        pp_i = [0]
        for n in (range(4) if stage >= 4 else []):
            g = GG[n]
            ensure_loaded(g)
            for t in range(8):
                i = zi[0] % 2
                zi[0] += 1
                for fc_ in range(16):
                    MM(zps[i][:, :], hnT[:, fc_, t * 128:(t + 1) * 128], wbuf[g % 2][:, fc_, :], fc_ == 0, fc_ == 15,
                       [("wbuf", g % 2, fc_ // 4), ("hnT", t)], [("zps", i)], fc_ == 15)
                pi = pp_i[0] % 2
                pp_i[0] += 1
                pps = (pA, pB)[pi]
                for k2 in range(2):
                    MM(pps[:, :], pTb[:, k2, t * 128:(t + 1) * 128], wpp[:, k2, n * 512:(n + 1) * 512], k2 == 0, k2 == 1,
                       ["pTb", "wpp"], [("pps", pi)], k2 == 1)
                ACT(sig[pi][:], zps[i][:, :], AF.Sigmoid, [("zps", i)], [("sig", pi)])
                TT(ot[pi][:], pps[:, :], sig[pi][:], ALU.mult, [("pps", pi), ("sig", pi)], [("ot", pi)])
                TT(ot[pi][:], ot[pi][:], h1[:, t, n * 512:(n + 1) * 512], ALU.add, [("ot", pi), ("h1", t), "R1"], [("ot", pi)])
                DMA("sp", y[t * 128:(t + 1) * 128, n * 512:(n + 1) * 512], ot[pi][:], [("ot", pi)], [])

        marks.append(len(S.stream["pe"]))
        build_nc.marks = marks
        final_waits = [(k, c) for k, c in S.cnt.items() if isinstance(k, tuple) and c > 0]

        block = es.enter_context(nc.Block())

        @block.sync
        def _(e):
            S.emit("sp", e)
            for k, c in final_waits:
                e.wait_ge(S.sem[k], c)

        @block.scalar
        def _(e):
            S.emit("act", e)

        @block.vector
        def _(e):
            S.emit("dve", e)

        @block.gpsimd
        def _(e):
            S.emit("pool", e)

        @block.tensor
        def _(e):
            S.emit("pe", e)
    return nc


def _host_tables(s2, attn_norm, ple_norm, ret_gn, diff_qn, diff_kn, lq1, lk1, lq2, lk2, subln):
    cstv = np.zeros((128, NCST), np.float32)

    def put(name, arr):
        o, w = _CST[name]
        cstv[:, o:o + w] = np.asarray(arr, np.float32).reshape(128, w)
    put("an_g", attn_norm.reshape(16, 128).T)
    put("pn_g", ple_norm.reshape(16, 128).T)
    put("retgn", ret_gn.reshape(8, 128).T)
    put("subln", subln.reshape(128, 1))
    put("qn2", np.concatenate([diff_qn, diff_qn]).reshape(128, 1))
    put("kn2", np.concatenate([diff_kn, diff_kn]).reshape(128, 1))
    for n, v in (("lq1", lq1), ("lk1", lk1), ("lq2", lq2), ("lk2", lk2)):
        put(n, np.broadcast_to(v.reshape(1, 64), (128, 64)))
    inv_freq = 10000.0 ** (-np.arange(32, dtype=np.float64) / 32.0)
    ctx = np.arange(2048)
    pos = (ctx if s2 == 1 else np.maximum(ctx - 1024, 0)).astype(np.float64)
    ang = pos[:, None] * inv_freq[None, :]
    cos = np.cos(ang).astype(np.float32).reshape(16, 128, 32).transpose(1, 0, 2)
    sin = np.sin(ang).astype(np.float32).reshape(16, 128, 32).transpose(1, 0, 2)
    put("cos", cos.reshape(128, 512))
    put("sin", sin.reshape(128, 512))
    log_g = np.log1p(-np.exp2(-5.0 - np.arange(8, dtype=np.float64)))
    idx = np.arange(128)
    kj = idx[:, None]
    qi = idx[None, :]
    mask = ~((kj >= 64) & (qi < 64))
    DT = np.exp(np.abs(qi - kj)[None] * log_g[:, None, None]) * mask[None] * 0.125
    put("DT", DT.transpose(1, 0, 2).reshape(128, 1024))
    xi = np.exp((idx + 1.0)[None, :] * log_g[:, None])
    XiT = np.zeros((128, 4, 128))
    for j in range(4):
        XiT[0:64, j, :] = xi[2 * j][None, :]
        XiT[64:128, j, :] = xi[2 * j + 1][None, :]
    put("XiT", XiT.reshape(128, 512))
    zeta = np.exp((127.0 - idx)[:, None] * log_g[None, :]) * 0.125
    put("zeta", zeta)
    g128 = np.exp(128.0 * log_g)
    G = np.zeros((128, 4))
    for j in range(4):
        G[0:64, j] = g128[2 * j]
        G[64:128, j] = g128[2 * j + 1]
    put("g128", G)
    flag = np.ones((128, 16), np.float32)
    if s2 == 0:
        flag[:, 0:8] = 0.0
    put("flag", flag)
    put("ident", np.eye(128, dtype=np.float32))
    return cstv


def _perm_w_in(w):
    o1, o2, o3, o4, o5, o6, o7 = 512, 1024, 2048, 3072, 4096, 5120, 6144
    cols = []
    for j in range(4):
        hs = (2 * j, 2 * j + 1)
        for h in hs:
            cols.append(np.arange(h * 64, (h + 1) * 64))
        for h in hs:
            cols.append(o1 + np.arange(h * 64, (h + 1) * 64))
        for h in hs:
            cols.append(o2 + np.arange(h * 128, (h + 1) * 128))
        if j % 2 == 0:
            for h in range(2 * j, 2 * j + 4):
                cols.append(o3 + np.arange(h * 128, (h + 1) * 128))
    for h in range(8):
        cols.append(o4 + np.arange(h * 128, (h + 1) * 128))
        cols.append(o5 + np.arange(h * 128, (h + 1) * 128))
        cols.append(o6 + np.arange(h * 128, (h + 1) * 128))
        cols.append(o7 + np.arange(h * 128, (h + 1) * 128))
    cols = np.concatenate(cols)
    assert cols.shape[0] == 7168 and np.unique(cols).shape[0] == 7168
    return np.ascontiguousarray(w[:, cols])


_NC_CACHE = {}


def make_in_maps(x, p, attn_norm, w_in, ret_gn, diff_qn, diff_kn, diff_lq1, diff_lk1, diff_lq2, diff_lk2,
                 diff_subln, w_out, ple_norm, w_ple_gate, w_ple_proj, batches=range(4)):
    f = lambda a: np.asarray(a, np.float32)
    x, p = f(x), f(p)
    w_in_p = _perm_w_in(f(w_in)[0])
    w_out_ = np.ascontiguousarray(f(w_out)[0])
    w_gate_ = np.ascontiguousarray(f(w_ple_gate)[0])
    w_pp_ = np.ascontiguousarray(f(w_ple_proj)[0])
    pn_rep_ = np.ascontiguousarray(np.broadcast_to(f(ple_norm)[0].reshape(1, 2048), (128, 2048)))
    in_maps = []
    for b in batches:
        for s2 in range(2):
            if s2 == 1:
                xcv = np.ascontiguousarray(x[b])
            else:
                xcv = np.concatenate([np.zeros((1024, 2048), np.float32), x[b, 0:1024]], axis=0)
            cstv = _host_tables(s2, f(attn_norm)[0], f(ple_norm)[0], f(ret_gn)[0], f(diff_qn)[0], f(diff_kn)[0],
                                f(diff_lq1)[0], f(diff_lk1)[0], f(diff_lq2)[0], f(diff_lk2)[0], f(diff_subln)[0])
            in_maps.append({
                "xc": xcv,
                "xcT": np.ascontiguousarray(xcv.T),
                "pT": np.ascontiguousarray(p[0, b, s2 * 1024:(s2 + 1) * 1024, :].T),
                "w_in": w_in_p, "w_out": w_out_, "w_gate": w_gate_, "w_pp": w_pp_,
                "cst": cstv,
                "pn_rep": pn_rep_,
            })
    return in_maps


def kernel(**inputs):
    in_maps = make_in_maps(**inputs)
    if "nc" not in _NC_CACHE:
        _NC_CACHE["nc"] = build_nc()
    res = run_bass_kernel_spmd(_NC_CACHE["nc"], in_maps, core_ids=list(range(8)))
    out = np.zeros((4, 2048, 2048), np.float32)
    i = 0
    for b in range(4):
        for s2 in range(2):
            out[b, s2 * 1024:(s2 + 1) * 1024, :] = res.results[i]["y"]
            i += 1
    return out
```
